# Optimizing a Trainium2 kernel written in Bass

```python
import jax, jax.numpy as jnp
from jax import lax
import numpy as np

D_MODEL = 1024
BATCH = 8
SEQ = 4096
DEPTH = 1

EPS = 1e-6
ROPE_THETA = 10000.0
BLOCK = 128

HEAD_DIM = 64
SWA_HEADS = 8
SWA_KV_HEADS = 2
SWA_GROUP = SWA_HEADS // SWA_KV_HEADS
WINDOW = 128

MLA_HEADS = 8
MLA_NOPE_DIM = 64
MLA_ROPE_DIM = 32
MLA_V_DIM = 64
MLA_QK_DIM = MLA_NOPE_DIM + MLA_ROPE_DIM
Q_LORA_RANK = 384
KV_LORA_RANK = 256

D_FF = -(-8 * D_MODEL // (3 * 256)) * 256

IN_SIZES = [
    SWA_HEADS * HEAD_DIM,
    SWA_KV_HEADS * HEAD_DIM,
    SWA_KV_HEADS * HEAD_DIM,
    Q_LORA_RANK,
    KV_LORA_RANK,
    MLA_ROPE_DIM,
    D_MODEL,
    D_MODEL,
]
IN_WIDTH = int(sum(IN_SIZES))
IN_OFFSETS = [int(v) for v in np.cumsum(IN_SIZES)[:-1]]

kernel_name = "hybrid_swa_sink_mla_gated_block"


def rmsnorm(x, g):
    xf = x.astype(jnp.float32)
    xf = xf * lax.rsqrt(jnp.mean(xf * xf, axis=-1, keepdims=True) + EPS)
    return (xf * g.astype(jnp.float32)).astype(x.dtype)


def rope_tables(seq, dim):
    inv = ROPE_THETA ** (-jnp.arange(0, dim, 2, dtype=jnp.float32) / dim)
    ang = jnp.arange(seq, dtype=jnp.float32)[:, None] * inv[None, :]
    return jnp.cos(ang)[:, None, :], jnp.sin(ang)[:, None, :]


def apply_rope(x, cos, sin):
    xf = x.astype(jnp.float32)
    x1, x2 = jnp.split(xf, 2, axis=-1)
    out = jnp.concatenate([x1 * cos - x2 * sin, x2 * cos + x1 * sin], axis=-1)
    return out.astype(x.dtype)


def swa_sink_attention(q, k, v, sinks):
    B, S = q.shape[0], q.shape[1]
    nb = S // BLOCK
    qb = q.reshape(B, nb, BLOCK, SWA_KV_HEADS, SWA_GROUP, HEAD_DIM)

    def band(t):
        tp = jnp.pad(t, ((0, 0), (BLOCK, 0), (0, 0), (0, 0)))
        tb = tp.reshape(B, nb + 1, BLOCK, SWA_KV_HEADS, HEAD_DIM)
        return jnp.concatenate([tb[:, :-1], tb[:, 1:]], axis=2)

    kw, vw = band(k), band(v)
    s = jnp.einsum('bnqhgd,bnkhd->bnhgqk', qb, kw,
                   preferred_element_type=jnp.float32) * (HEAD_DIM ** -0.5)
    qi = jnp.arange(BLOCK)[:, None]
    kj = jnp.arange(2 * BLOCK)[None, :]
    diff = qi - kj + BLOCK
    band_ok = (diff >= 0) & (diff < WINDOW)
    kpos = jnp.arange(nb)[:, None] * BLOCK + kj - BLOCK
    mask = band_ok[None] & (kpos >= 0)[:, None, :]
    s = jnp.where(mask[None, :, None, None], s, -jnp.inf)
    sink = sinks.astype(jnp.float32).reshape(1, 1, SWA_KV_HEADS, SWA_GROUP, 1, 1)
    m = jnp.maximum(jnp.max(s, axis=-1, keepdims=True), sink)
    p = jnp.exp(s - m)
    denom = jnp.sum(p, axis=-1, keepdims=True) + jnp.exp(sink - m)
    p = (p / denom).astype(v.dtype)
    o = jnp.einsum('bnhgqk,bnkhd->bnqhgd', p, vw)
    return o.reshape(B, S, SWA_HEADS * HEAD_DIM)


def mla_attention(q_nope, q_rope, k_nope, k_rope, v):
    B, S = q_nope.shape[0], q_nope.shape[1]
    nb = S // BLOCK
    scale = MLA_QK_DIM ** -0.5
    kpos = jnp.arange(S)

    def to_blocks(t):
        return jnp.moveaxis(t.reshape(B, nb, BLOCK, *t.shape[2:]), 1, 0)

    def one_block(args):
        qn, qr, n = args
        s = (jnp.einsum('bqhd,bkhd->bhqk', qn, k_nope, preferred_element_type=jnp.float32)
             + jnp.einsum('bqhr,bkr->bhqk', qr, k_rope, preferred_element_type=jnp.float32)) * scale
        qpos = n * BLOCK + jnp.arange(BLOCK)
        causal = kpos[None, :] <= qpos[:, None]
        s = jnp.where(causal[None, None], s, -jnp.inf)
        p = jax.nn.softmax(s, axis=-1).astype(v.dtype)
        return jnp.einsum('bhqk,bkhd->bqhd', p, v)

    o = lax.map(one_block, (to_blocks(q_nope), to_blocks(q_rope), jnp.arange(nb)))
    return jnp.moveaxis(o, 0, 1).reshape(B, S, MLA_HEADS * MLA_V_DIM)


def setup_inputs(seed: int = 0) -> dict:
    key = jax.random.key(seed)
    ks = jax.random.split(key, 17)
    f32 = jnp.float32

    def w(k, shape, fan_in):
        return jax.random.normal(k, shape, f32) * (fan_in ** -0.5)

    def gain(k, shape):
        return 1.0 + 0.02 * jax.random.normal(k, shape, f32)

    L = DEPTH
    return {
        "x": jax.random.normal(ks[0], (BATCH, SEQ, D_MODEL), f32),
        "mix_norm_g": gain(ks[1], (L, D_MODEL)),
        "w_in": w(ks[2], (L, D_MODEL, IN_WIDTH), D_MODEL),
        "swa_sinks": 0.5 * jax.random.normal(ks[3], (L, SWA_HEADS), f32),
        "q_norm_g": gain(ks[4], (L, Q_LORA_RANK)),
        "w_uq": w(ks[5], (L, Q_LORA_RANK, MLA_HEADS * MLA_QK_DIM), Q_LORA_RANK),
        "kv_norm_g": gain(ks[6], (L, KV_LORA_RANK)),
        "w_ukv": w(ks[7], (L, KV_LORA_RANK, MLA_HEADS * (MLA_NOPE_DIM + MLA_V_DIM)), KV_LORA_RANK),
        "w_o_swa": w(ks[8], (L, SWA_HEADS * HEAD_DIM, D_MODEL), SWA_HEADS * HEAD_DIM),
        "w_o_mla": w(ks[9], (L, MLA_HEADS * MLA_V_DIM, D_MODEL), MLA_HEADS * MLA_V_DIM),
        "w_out": w(ks[10], (L, D_MODEL, D_MODEL), D_MODEL),
        "ffn_norm_g": gain(ks[11], (L, D_MODEL)),
        "w_gate": w(ks[12], (L, D_MODEL, D_FF), D_MODEL),
        "w_up": w(ks[13], (L, D_MODEL, D_FF), D_MODEL),
        "w_down": w(ks[14], (L, D_FF, D_MODEL), D_FF),
        "final_norm_g": gain(ks[15], (D_MODEL,)),
    }


def reference(x, mix_norm_g, w_in, swa_sinks, q_norm_g, w_uq, kv_norm_g, w_ukv,
              w_o_swa, w_o_mla, w_out, ffn_norm_g, w_gate, w_up, w_down, final_norm_g):
    B, S = x.shape[0], x.shape[1]
    cos_a, sin_a = rope_tables(S, HEAD_DIM)
    cos_b, sin_b = rope_tables(S, MLA_ROPE_DIM)

    for l in range(DEPTH):
        h = rmsnorm(x, mix_norm_g[l])
        proj = h @ w_in[l]
        qa, ka, va, q_lat, kv_lat, k_r, g_a, g_b = jnp.split(proj, IN_OFFSETS, axis=-1)

        qa = apply_rope(qa.reshape(B, S, SWA_HEADS, HEAD_DIM), cos_a, sin_a)
        ka = apply_rope(ka.reshape(B, S, SWA_KV_HEADS, HEAD_DIM), cos_a, sin_a)
        va = va.reshape(B, S, SWA_KV_HEADS, HEAD_DIM)
        o_a = swa_sink_attention(qa, ka, va, swa_sinks[l])

        cq = rmsnorm(q_lat, q_norm_g[l])
        qb = (cq @ w_uq[l]).reshape(B, S, MLA_HEADS, MLA_QK_DIM)
        q_nope, q_rope = jnp.split(qb, [MLA_NOPE_DIM], axis=-1)
        q_rope = apply_rope(q_rope, cos_b, sin_b)
        ckv = rmsnorm(kv_lat, kv_norm_g[l])
        kvb = (ckv @ w_ukv[l]).reshape(B, S, MLA_HEADS, MLA_NOPE_DIM + MLA_V_DIM)
        k_nope, vb = jnp.split(kvb, [MLA_NOPE_DIM], axis=-1)
        k_rope = apply_rope(k_r[:, :, None, :], cos_b, sin_b)[:, :, 0, :]
        o_b = mla_attention(q_nope, q_rope, k_nope, k_rope, vb)

        y = jax.nn.sigmoid(g_a) * (o_a @ w_o_swa[l]) + jax.nn.sigmoid(g_b) * (o_b @ w_o_mla[l])
        x = x + y @ w_out[l]

        h = rmsnorm(x, ffn_norm_g[l])
        x = x + (jax.nn.silu(h @ w_gate[l]) * (h @ w_up[l])) @ w_down[l]

    return rmsnorm(x, final_norm_g)
```

```python
import contextlib
import numpy as np
import ml_dtypes
import concourse.bass as bass
import concourse.mybir as mybir
from concourse.bass_utils import run_bass_kernel_spmd

F32, BF16 = mybir.dt.float32, mybir.dt.bfloat16
AF = mybir.ActivationFunctionType
ALU = mybir.AluOpType
AX = mybir.AxisListType

SEQ = 4096
DM = 1024
import os
NCH = int(os.environ.get('K_NCH', '8'))
CH = 512
DFF = 2816
NFF = 22
SCALE_A = 64 ** -0.5
SCALE_B = 96 ** -0.5
NEG = -30000.0
EPS = 1e-6

ENGS = ("pe", "act", "dve", "pool", "sp")
N_DMA_SEMS = 24


class _Op:
    __slots__ = ("eng", "fn", "deps", "is_dma", "signal", "seq", "dsem", "dval", "pre_wait")

    def __init__(self, eng, fn, is_dma):
        self.eng = eng
        self.fn = fn
        self.is_dma = is_dma
        self.deps = []
        self.signal = False
        self.seq = None
        self.dsem = None
        self.dval = None
        self.pre_wait = None


class Sched:
    def __init__(self, nc):
        self.nc = nc
        self.ops = {e: [] for e in ENGS}
        self.all = []
        self.last_w = {}
        self.readers = {}
        self.dma_i = 0
        self.dma_ip = 0
        self.dma_last = [None] * N_DMA_SEMS
        self.dma_val = [0] * N_DMA_SEMS
        self.last_eng = {e: None for e in ENGS}
        self.pending_barrier = {e: [] for e in ENGS}

    def barrier(self):
        deps = [o for o in self.last_eng.values() if o is not None]
        deps += [o for o in self.dma_last if o is not None]
        for d in deps:
            d.signal = True
        for e in ENGS:
            self.pending_barrier[e] = list(deps)

    def op(self, eng, fn, reads=(), writes=(), dma=False):
        o = _Op(eng, fn, dma)
        psr = [k for k in reads if isinstance(k, tuple) and k[0] in ("ps", "pT")]
        if psr:
            reads = [k for k in reads if k not in psr]
            writes = list(writes) + psr
        deps = set()
        for k in reads:
            w = self.last_w.get(k)
            if w is not None:
                deps.add(w)
        for k in writes:
            w = self.last_w.get(k)
            if w is not None:
                deps.add(w)
            for r in self.readers.get(k, ()):
                deps.add(r)
        if self.pending_barrier[eng]:
            deps.update(self.pending_barrier[eng])
            self.pending_barrier[eng] = []
        for d in deps:
            if d.eng == "pe" and eng == "pe" and not d.is_dma and not dma:
                continue
            o.deps.append(d)
            d.signal = True
        for k in writes:
            self.last_w[k] = o
            self.readers[k] = []
        for k in reads:
            if k in writes:
                continue
            self.readers.setdefault(k, []).append(o)
        if dma:
            if eng == "pool":
                s = 16 + self.dma_ip % 8
                self.dma_ip += 1
            else:
                s = self.dma_i % 16
                self.dma_i += 1
            o.pre_wait = self.dma_last[s]
            self.dma_val[s] += 16
            o.dsem = s
            o.dval = self.dma_val[s]
            self.dma_last[s] = o
        else:
            self.last_eng[eng] = o
        self.ops[eng].append(o)
        self.all.append(o)
        return o

    def emit(self):
        nc = self.nc
        cnt = {e: 0 for e in ENGS}
        for o in self.all:
            if (not o.is_dma) and o.signal:
                cnt[o.eng] += 1
                o.seq = cnt[o.eng]
        dma_val = self.dma_val
        with contextlib.ExitStack() as st:
            esem = {e: st.enter_context(nc.semaphore("s_" + e)) for e in ENGS}
            dsem = [st.enter_context(nc.semaphore("d%d" % i)) for i in range(N_DMA_SEMS)]
            block = st.enter_context(nc.Block())

            def run(engname, eng):
                waited = {}

                def wait(key, sem, val):
                    if waited.get(key, 0) >= val:
                        return
                    waited[key] = val
                    eng.wait_ge(sem, val)

                for o in self.ops[engname]:
                    for d in o.deps:
                        if d.is_dma:
                            wait(("d", d.dsem), dsem[d.dsem], d.dval)
                        else:
                            wait(("e", d.eng), esem[d.eng], d.seq)
                    if o.is_dma:
                        p = o.pre_wait
                        if p is not None:
                            wait(("d", p.dsem), dsem[p.dsem], p.dval)
                        ins = o.fn(eng)
                        ins.then_inc(dsem[o.dsem], 16)
                    else:
                        ins = o.fn(eng)
                        if o.signal:
                            ins.then_inc(esem[engname], 1)
                if engname == "sp":
                    for s in range(N_DMA_SEMS):
                        if dma_val[s] > 0:
                            wait(("d", s), dsem[s], dma_val[s])

            @block.tensor
            def _(e):
                run("pe", e)

            @block.scalar
            def _(e):
                run("act", e)

            @block.vector
            def _(e):
                run("dve", e)

            @block.gpsimd
            def _(e):
                run("pool", e)

            @block.sync
            def _(e):
                run("sp", e)


def _interleave(a, b):
    if not b:
        return list(a)
    out = []
    step = max(1, len(a) // (len(b) + 1))
    bi = 0
    for i, it in enumerate(a):
        out.append(it)
        if bi < len(b) and (i + 1) % step == 0:
            out.append(b[bi])
            bi += 1
    out.extend(b[bi:])
    return out


def build_nc(debug=False, phases=(1, 2, 3)):
    nc = bass.Bass("TRN2", target_bir_lowering=False)

    def din(name, shape, dt=F32):
        return nc.dram_tensor(name, list(shape), dt, kind="ExternalInput").ap()

    x = din("x", [SEQ, DM])
    mix_g = din("mix_norm_g", [DM])
    w_in = din("w_in", [DM, 3488])
    q_g = din("q_norm_g", [384])
    w_uq = din("w_uq", [384, 768])
    kv_g = din("kv_norm_g", [256])
    w_ukv = din("w_ukv", [256, 1024])
    w_osa = din("w_o_swa", [512, DM])
    w_osb = din("w_o_mla", [512, DM])
    w_out = din("w_out", [DM, DM])
    ffn_g = din("ffn_norm_g", [DM])
    w_gate = din("w_gate", [DM, DFF])
    w_up = din("w_up", [DM, DFF])
    w_down = din("w_down", [DFF, DM])
    fin_g = din("final_norm_g", [DM])
    sinkrep = din("sinkrep", [1, 8 * 128])
    sinkb = din("sinkb", [128, 8])
    cosA = din("cosA", [64, SEQ])
    sinA = din("sinA", [64, SEQ])
    cosB = din("cosB", [32, SEQ])
    sinB = din("sinB", [32, SEQ])
    ident_d = din("ident", [128, 128], BF16)
    mdiag_d = din("mdiag4", [128, 4, 128], BF16)
    mprev_d = din("mprev4", [128, 4, 128], BF16)
    e128_d = din("e128", [1, 2, 128], BF16)
    perm_d = din("ropeperm", [128, 64], BF16)
    onesrow_d = din("onesrow", [1, 8, SEQ], BF16)

    out = nc.dram_tensor("out", [SEQ, DM], F32, kind="ExternalOutput").ap()

    skind = "ExternalOutput" if debug else "Internal"

    def dscr(name, shape, dt):
        return nc.dram_tensor(name, list(shape), dt, kind=skind).ap()

    QaTd = dscr("QaTd", [64, 8, SEQ], BF16)
    KaTd = dscr("KaTd", [64, 2, SEQ], BF16)
    Vad = dscr("Vad", [128, 32, 3, 64], BF16)
    QTd = dscr("QTd", [96, 8, SEQ], BF16)
    KTd = dscr("KTd", [96, 8, SEQ], BF16)
    Vd = dscr("Vd", [128, 32, 4, 3, 64], BF16)
    Gad = dscr("Gad", [128, 8, SEQ], BF16)
    Gbd = dscr("Gbd", [128, 8, SEQ], BF16)
    x1d = dscr("x1d", [SEQ, DM], F32)
    if debug:
        dOaT = dscr("dOaT", [128, 4, CH], BF16)
        dObT = dscr("dObT", [128, 4, CH], BF16)
        dyT = dscr("dyT", [128, 8, CH], BF16)
        dVc = dscr("dVc", [128, 4, 4, 192], BF16)
        dKT = dscr("dKT", [128, 8, CH], BF16)
        dQT = dscr("dQT", [128, 8, CH], BF16)

    S = Sched(nc)
    top = contextlib.ExitStack()

    def sbt(stack, name, shape, dt):
        return stack.enter_context(nc.sbuf_tensor("sb_" + name, list(shape), dt))

    pp = [top.enter_context(nc.psum_tensor("pp%d" % i, [128, 2, 512], F32)) for i in range(3)]
    ps = [pp[i // 2][:, i % 2, :] for i in range(6)]
    ps += [top.enter_context(nc.psum_tensor("ps%d" % i, [128, 512], F32))[:] for i in (6, 7)]
    pT = [ps[6 + i].bitcast(BF16).rearrange("p (k c) -> p k c", k=8) for i in range(2)]

    ident = sbt(top, "ident", [128, 128], BF16)
    ones_bf = sbt(top, "ones_bf", [128, 128], BF16)
    statAq = sbt(top, "statAq", [128, 64], F32)
    statAk = sbt(top, "statAk", [128, 16], F32)
    statBq = sbt(top, "statBq", [128, 64], F32)
    statBk = sbt(top, "statBk", [128, 64], F32)
    ss = sbt(top, "ss", [128, 2], F32)
    rstd = sbt(top, "rstd", [128, 2], F32)

    S.op("sp", lambda e: e.dma_start(out=ident[:], in_=ident_d), writes=["ident"], dma=True)
    S.op("pool", lambda e: e.memset(ones_bf[:], 1.0), writes=["ones_bf"])
    for stt in (statAq, statAk, statBq, statBk):
        S.op("pool", lambda e, stt=stt: e.memset(stt[:], 0.0), writes=[("statinit", id(stt))])

    bank_ctr = [0]

    def nb():
        b = bank_ctr[0] % 6
        bank_ctr[0] += 1
        return b

    def PS(b):
        return ("ps", b)

    blk_ctr = [0]

    def norm_block(src, t0, xb, junk, xs, hT_ap, hT_key, gT):
        b = blk_ctr[0] % 2
        blk_ctr[0] += 1

        def prep():
            S.op("sp", lambda e: e.dma_start(out=xb[b][:], in_=src[t0:t0 + 128, :]),
                 writes=[("xb", b)], dma=True)
            S.op("act", lambda e: e.activation(out=junk[:], in_=xb[b][:], func=AF.Square,
                                               accum_out=ss[:, b:b + 1]),
                 reads=[("xb", b)], writes=["junk", ("ss", b)])
            S.op("act", lambda e: e.activation(out=rstd[:, b:b + 1], in_=ss[:, b:b + 1], func=AF.Ln,
                                               scale=1.0 / DM, bias=EPS),
                 reads=[("ss", b)], writes=[("rstd", b)])
            S.op("act", lambda e: e.activation(out=rstd[:, b:b + 1], in_=rstd[:, b:b + 1], func=AF.Exp,
                                               scale=-0.5),
                 reads=[("rstd", b)], writes=[("rstd", b)])
            S.op("dve", lambda e: e.tensor_scalar(out=xs[b][:], in0=xb[b][:], scalar1=rstd[:, b:b + 1],
                                                  scalar2=None, op0=ALU.mult),
                 reads=[("xb", b), ("rstd", b)], writes=[("xs", b)])

        def xpose():
            for kc in range(8):
                S.op("pe", lambda e, kc=kc: e.transpose(out=pT[b][:, kc, :],
                                                        in_=xs[b][:, kc * 128:(kc + 1) * 128],
                                                        identity=ident[:]),
                     reads=[("xs", b), "ident"], writes=[PS(6 + b)])
            S.op("dve", lambda e: e.tensor_tensor(out=hT_ap, in0=pT[b],
                                                  in1=gT[:].unsqueeze(2).to_broadcast([128, 8, 128]),
                                                  op=ALU.mult),
                 reads=[PS(6 + b), "gT"], writes=[hT_key])
        return prep, xpose

    def norm_sched(blocks):
        p = [b_[0] for b_ in blocks]
        x_ = [b_[1] for b_ in blocks]
        return [p[0], p[1], x_[0], p[2], x_[1], p[3], x_[2], x_[3]]

    def mm_group(out_ap, pairs, reads, bank, extra=()):
        n = len(pairs) + len(extra)
        i = 0
        for (l, r) in list(pairs) + list(extra):
            S.op("pe", lambda e, l=l, r=r, i=i: e.matmul(out_ap, lhsT=l, rhs=r,
                                                         start=(i == 0), stop=(i == n - 1)),
                 reads=reads, writes=[PS(bank)])
            i += 1

    if 1 in phases:
        st = contextlib.ExitStack()
        Win = sbt(st, "Win", [128, 8, 3488], BF16)
        Wsw = sbt(st, "Wsw", [128, 8, 672], BF16)
        Wuq = sbt(st, "Wuq", [128, 3, 768], BF16)
        Wuqsw = sbt(st, "Wuqsw", [128, 3, 256], BF16)
        Wukv = sbt(st, "Wukv", [128, 2, 1024], BF16)
        gmix = sbt(st, "gmix", [128, 8], F32)
        gq = sbt(st, "gq", [128, 3], F32)
        gkv = sbt(st, "gkv", [128, 2], F32)
        xb = [sbt(st, "xb%d" % i, [128, DM], F32) for i in range(2)]
        junk = sbt(st, "junk", [128, DM], BF16)
        xs = [sbt(st, "xs%d" % i, [128, DM], BF16) for i in range(2)]
        hT = [sbt(st, "hT%d" % i, [128, 8, CH], BF16) for i in range(2)]
        cA2 = [sbt(st, "cA%d" % i, [64, CH], F32) for i in range(2)]
        sA2 = [sbt(st, "sA%d" % i, [64, CH], F32) for i in range(2)]
        cB2 = [sbt(st, "cB%d" % i, [128, CH], F32) for i in range(2)]
        sB2 = [sbt(st, "sB%d" % i, [128, CH], F32) for i in range(2)]

        def load_tables_for(cc):
            tsl_ = slice(cc * CH, (cc + 1) * CH)
            i_ = cc % 2
            S.op("sp", lambda e: e.dma_start(out=cA2[i_][:], in_=cosA[:, tsl_]), writes=[("cA", i_)], dma=True)
            S.op("sp", lambda e: e.dma_start(out=sA2[i_][:], in_=sinA[:, tsl_]), writes=[("sA", i_)], dma=True)
            S.op("sp", lambda e: e.dma_start(out=cB2[i_][64:96, :], in_=cosB[:, tsl_]), writes=[("cB", i_)], dma=True)
            S.op("sp", lambda e: e.dma_start(out=sB2[i_][64:96, :], in_=sinB[:, tsl_]), writes=[("sB", i_)], dma=True)
        QaT = sbt(st, "QaT", [64, 8, CH], BF16)
        KaT = sbt(st, "KaT", [64, 2, CH], BF16)
        Va = sbt(st, "Va", [128, 4, 3, 64], BF16)
        qlf = sbt(st, "qlf", [128, 3, CH], F32)
        sq = sbt(st, "sq", [128, 3, CH], BF16)
        rq = sbt(st, "rq", [128, CH], F32)
        cqT = sbt(st, "cqT", [128, 3, CH], BF16)
        ckvT = sbt(st, "ckvT", [128, 2, CH], BF16)
        QT = sbt(st, "QT", [96, 8, CH], BF16)
        KT = sbt(st, "KT", [96, 8, CH], BF16)
        Vb = sbt(st, "Vb", [128, 4, 4, 3, 64], BF16)
        Ga = sbt(st, "Ga", [128, 8, CH], BF16)
        Gb = sbt(st, "Gb", [128, 8, CH], BF16)
        t1 = sbt(st, "t1", [128, CH], F32)
        t2 = sbt(st, "t2", [128, CH], F32)
        sq8 = sbt(st, "sq8", [128, 8, CH], BF16)
        Pm = sbt(st, "Pm", [128, 64], BF16)
        qbf = [sbt(st, "qbf%d" % i, [128, CH], BF16) for i in range(2)]
        S.op("sp", lambda e: e.dma_start(out=Pm[:], in_=perm_d), writes=["Pm"], dma=True)
        qctr = [0]
        pend = []

        WGRP = ((0, 768), (768, 1440), (1440, 2464), (2464, 3488))

        def WK(lo):
            return [("Win", gi) for gi, (a, b_) in enumerate(WGRP) if a <= lo < b_]
        for gi, (a, b_) in enumerate(WGRP):
            S.op("pool", lambda e, a=a, b_=b_: e.dma_start(
                out=Win[:, :, a:b_], in_=w_in[:, a:b_].rearrange("(kc p) n -> p kc n", p=128)),
                writes=[("Win", gi)], dma=True)
        S.op("pool", lambda e: e.dma_start(out=Wuq[:], in_=w_uq.rearrange("(i p) n -> p i n", p=128)),
             writes=["Wuq"], dma=True)
        S.op("pool", lambda e: e.dma_start(out=Wukv[:], in_=w_ukv.rearrange("(i p) n -> p i n", p=128)),
             writes=["Wukv"], dma=True)
        S.op("sp", lambda e: e.dma_start(out=gmix[:], in_=mix_g.rearrange("(kc p) -> p kc", p=128),
                                         allow_slow_non_contiguous=True), writes=["gT"], dma=True)
        S.op("sp", lambda e: e.dma_start(out=gq[:], in_=q_g.rearrange("(kc p) -> p kc", p=128),
                                         allow_slow_non_contiguous=True), writes=["gq"], dma=True)
        S.op("sp", lambda e: e.dma_start(out=gkv[:], in_=kv_g.rearrange("(kc p) -> p kc", p=128),
                                         allow_slow_non_contiguous=True), writes=["gkv"], dma=True)
        S.op("dve", lambda e: e.memset(sq8[:], 0.0), writes=["sq8"])
        S.op("dve", lambda e: e.memset(Va[:, :, 1, :], 1.0), writes=["Va"])
        for j in range(4):
            S.op("dve", lambda e, j=j: e.memset(Vb[:, j, :, 1, :], 1.0), writes=["Vb"])

        def norm_items(c):
            return norm_sched([norm_block(x, c * CH + j * 128, xb, junk, xs,
                                          hT[c % 2][:, :, j * 128:(j + 1) * 128], ("hT", c % 2, j), gmix)
                               for j in range(4)])

        def proj_items(c):
            items = []
            h_ = hT[c % 2]
            hK = [("hT", c % 2, j) for j in range(4)]
            tsl = slice(c * CH, (c + 1) * CH)

            def w_pairs(W, lo, hi):
                return [(W[:, kc, lo:hi], h_[:, kc, :]) for kc in range(8)]

            cA, sA, cB, sB = cA2[c % 2], sA2[c % 2], cB2[c % 2], sB2[c % 2]
            kcA, ksA, kcB, ksB = ("cA", c % 2), ("sA", c % 2), ("cB", c % 2), ("sB", c % 2)

            def load_tables():
                if c + 1 < NCH:
                    load_tables_for(c + 1)
            items.append(load_tables)

            def rope_evac(b0, b1, prt, ctab, stab, ckey, skey, out_ap, out_key):
                S.op("dve", lambda e: e.tensor_tensor(out=t1[prt, :], in0=ps[b0][prt, :], in1=ctab[prt, :],
                                                      op=ALU.mult),
                     reads=[PS(b0), ckey], writes=["t1"])
                S.op("dve", lambda e: e.tensor_tensor(out=t2[prt, :], in0=ps[b1][prt, :], in1=stab[prt, :],
                                                      op=ALU.mult),
                     reads=[PS(b1), skey], writes=["t2"])
                S.op("dve", lambda e: e.tensor_tensor(out=out_ap, in0=t1[prt, :], in1=t2[prt, :], op=ALU.add),
                     reads=["t1", "t2"], writes=[out_key])

            def rope_perm(b0, prt, nrow, ctab, stab, ckey, skey, out_ap, out_key, after=None):
                qi = qctr[0] % 2
                qctr[0] += 1
                b1 = nb()
                S.op("act", lambda e: e.activation(out=qbf[qi][prt, :], in_=ps[b0][prt, :], func=AF.Copy),
                     reads=[PS(b0)], writes=[("qbf", qi)])

                def fin():
                    S.op("pe", lambda e: e.matmul(ps[b1][prt, :], lhsT=Pm[prt, 0:nrow], rhs=qbf[qi][prt, :],
                                                  start=True, stop=True),
                         reads=[("qbf", qi), "Pm"], writes=[PS(b1)])
                    rope_evac(b0, b1, prt, ctab, stab, ckey, skey, out_ap, out_key)
                    if after is not None:
                        after()
                pend.append(fin)

            for h in range(8):
                def it(h=h):
                    b0 = nb()
                    mm_group(ps[b0][0:64, :], w_pairs(Win, h * 64, h * 64 + 64), hK + WK(0), b0)
                    rope_perm(b0, slice(0, 64), 64, cA, sA, kcA, ksA, QaT[:, h, :], "QaT")
                items.append(it)
            for g in range(2):
                def it(g=g):
                    b0 = nb()
                    mm_group(ps[b0][0:64, :], w_pairs(Win, 512 + g * 64, 512 + g * 64 + 64), hK + WK(0), b0)
                    rope_perm(b0, slice(0, 64), 64, cA, sA, kcA, ksA, KaT[:, g, :], "KaT")
                items.append(it)

            def it_va():
                b0 = nb()
                for j in range(4):
                    n = 8
                    for kc in range(8):
                        S.op("pe", lambda e, j=j, kc=kc: e.matmul(
                            ps[b0][:, j * 128:(j + 1) * 128], lhsT=h_[:, kc, j * 128:(j + 1) * 128],
                            rhs=Win[:, kc, 640:768], start=(kc == 0), stop=(kc == 7)),
                            reads=hK + WK(0), writes=[PS(b0)])
                S.op("act", lambda e: e.activation(
                    out=Va[:, :, 0:3:2, :],
                    in_=ps[b0][:].rearrange("p (j g d) -> p j g d", j=4, g=2), func=AF.Copy),
                    reads=[PS(b0)], writes=["Va"])
                S.op("sp", lambda e: e.dma_start(out=Vad[:, 4 * c:4 * c + 4, :, :], in_=Va[:]),
                     reads=["Va"], writes=[("Vad", c)], dma=True)
            items.append(it_va)

            def latent(lo, ntile, gvec, gkey, outT, okey, dim):
                def it():
                    banks = []
                    for i in range(ntile):
                        b0 = nb()
                        banks.append(b0)
                        mm_group(ps[b0][:, :], w_pairs(Win, lo + i * 128, lo + (i + 1) * 128), hK + WK(lo), b0)
                        S.op("act", lambda e, i=i, b0=b0: e.activation(out=sq[:, i, :], in_=ps[b0][:, :],
                                                                        func=AF.Square),
                             reads=[PS(b0)], writes=[("sq", i)])
                        S.op("dve", lambda e, i=i, b0=b0: e.tensor_copy(out=qlf[:, i, :], in_=ps[b0][:, :]),
                             reads=[PS(b0)], writes=[("qlf", i)])
                    bs = nb()
                    mm_group(ps[bs][:, :], [(ones_bf[:, :], sq[:, i, :]) for i in range(ntile)],
                             [("sq", i) for i in range(ntile)] + ["ones_bf"], bs)
                    S.op("act", lambda e: e.activation(out=rq[:], in_=ps[bs][:, :], func=AF.Ln,
                                                       scale=1.0 / dim, bias=EPS),
                         reads=[PS(bs)], writes=["rq"])
                    S.op("act", lambda e: e.activation(out=rq[:], in_=rq[:], func=AF.Exp, scale=-0.5),
                         reads=["rq"], writes=["rq"])
                    for i in range(ntile):
                        S.op("dve", lambda e, i=i: e.scalar_tensor_tensor(
                            out=outT[:, i, :], in0=qlf[:, i, :], scalar=gvec[:, i:i + 1], in1=rq[:],
                            op0=ALU.mult, op1=ALU.mult),
                            reads=[("qlf", i), "rq", gkey], writes=[okey])
                return it
            items.append(latent(768, 3, gq, "gq", cqT, "cqT", 384.0))
            items.append(latent(1152, 2, gkv, "gkv", ckvT, "ckvT", 256.0))

            def it_kr():
                b0 = nb()
                mm_group(ps[b0][64:96, :], w_pairs(Win, 1408, 1440), hK + WK(1408), b0)

                def bcast():
                    S.op("act", lambda e: e.activation(
                        out=KT[64:96, :, :], in_=t1[64:96, :].unsqueeze(1).to_broadcast([32, 8, CH]),
                        func=AF.Copy),
                        reads=["t1"], writes=["KT"])
                rope_perm(b0, slice(64, 96), 32, cB, sB, kcB, ksB, t1[64:96, :], "t1", after=bcast)
            items.append(it_kr)

            for h in range(8):
                def it(h=h):
                    b0 = nb()
                    cq_pairs = lambda W, lo, hi: [(W[:, i, lo:hi], cqT[:, i, :]) for i in range(3)]
                    mm_group(ps[b0][0:64, :], cq_pairs(Wuq, h * 96, h * 96 + 64), ["cqT", "Wuq"], b0)
                    mm_group(ps[b0][64:96, :], cq_pairs(Wuq, h * 96 + 64, h * 96 + 96), ["cqT", "Wuq"], b0)
                    S.op("act", lambda e: e.activation(out=QT[0:64, h, :], in_=ps[b0][0:64, :], func=AF.Copy),
                         reads=[PS(b0)], writes=["QT"])
                    rope_perm(b0, slice(64, 96), 32, cB, sB, kcB, ksB, QT[64:96, h, :], "QT")
                items.append(it)

            for h in range(8):
                def it(h=h):
                    b0 = nb()
                    mm_group(ps[b0][0:64, :], [(Wukv[:, i, h * 128:h * 128 + 64], ckvT[:, i, :]) for i in range(2)],
                             ["ckvT", "Wukv"], b0)
                    S.op("act", lambda e: e.activation(out=KT[0:64, h, :], in_=ps[b0][0:64, :], func=AF.Copy),
                         reads=[PS(b0)], writes=["KT"])
                items.append(it)
            for j in range(4):
                def it(j=j):
                    b0 = nb()
                    wv = Wukv[:].rearrange("p k (h f) -> p k h f", f=128)
                    mm_group(ps[b0][:, :], [(ckvT[:, i, j * 128:(j + 1) * 128], wv[:, i, :, 64:128]) for i in range(2)],
                             ["ckvT", "Wukv"], b0)
                    pv4 = ps[b0][:].rearrange("p (pr par d) -> p pr par d", pr=4, par=2)
                    for par in range(2):
                        S.op("dve", lambda e, par=par: e.tensor_copy(
                            out=Vb[:, j, :, 2 * par, :], in_=pv4[:, :, par, :]),
                            reads=[PS(b0)], writes=["Vb"])
                items.append(it)

            for m in range(16):
                def it(m=m):
                    b0 = nb()
                    mm_group(ps[b0][:, :], w_pairs(Win, 1440 + m * 128, 1440 + (m + 1) * 128), hK + WK(1440 + m * 128), b0)
                    G = Ga if m < 8 else Gb
                    S.op("act", lambda e: e.activation(out=G[:, m % 8, :], in_=ps[b0][:, :], func=AF.Sigmoid),
                         reads=[PS(b0)], writes=["Ga" if m < 8 else "Gb"])
                items.append(it)

            def stats(T, tkey, nrow, nh, stat, col0):
                def prep():
                    S.op("act", lambda e: e.activation(out=sq8[0:nrow, 0:nh, :], in_=T[0:nrow, 0:nh, :],
                                                       func=AF.Square),
                         reads=[tkey], writes=["sq8"])

                def pe_():
                    for h in range(nh):
                        b0 = nb()
                        mm_group(ps[b0][:, :], [(ones_bf[0:nrow, :], sq8[0:nrow, h, :])], ["sq8", "ones_bf"], b0)
                        S.op("dve", lambda e, h=h, b0=b0: e.tensor_reduce(
                            out=stat[:, col0 + h:col0 + h + 1], in_=ps[b0][:, :], axis=AX.X, op=ALU.max),
                            reads=[PS(b0)], writes=[("stat", id(stat), col0 + h)])
                return prep, pe_
            stA = stats(QaT, "QaT", 64, 8, statAq, c * 8)
            stB = stats(KaT, "KaT", 64, 2, statAk, c * 2)
            stC = stats(QT, "QT", 96, 8, statBq, c * 8)
            stD = stats(KT, "KT", 96, 8, statBk, c * 8)

            def spill():
                for (dst, src, k, dk) in ((QaTd[:, :, tsl], QaT, "QaT", "QaTd"),
                                          (KaTd[:, :, tsl], KaT, "KaT", "KaTd"),
                                          (QTd[:, :, tsl], QT, "QT", "QTd"),
                                          (KTd[:, :, tsl], KT, "KT", "KTd"),
                                          (Gad[:, :, tsl], Ga, "Ga", "Gad"),
                                          (Gbd[:, :, tsl], Gb, "Gb", "Gbd")):
                    S.op("sp", lambda e, dst=dst, src=src: e.dma_start(out=dst, in_=src[:]),
                         reads=[k], writes=[(dk, c)], dma=True)
                for j in range(4):
                    S.op("dve", lambda e, j=j: e.memset(Vb[:, j, :, 1, :], 1.0), writes=["Vb"])
                S.op("sp", lambda e: e.dma_start(out=Vd[:, 4 * c:4 * c + 4, :, :, :], in_=Vb[:]),
                     reads=["Vb"], writes=[("Vd", c)], dma=True)
            (tab, swaq, swak, va_, latq, latkv, kr_, mlaq, knope, vmla, gates) = (
                items[0:1], items[1:9], items[9:11], items[11:12], items[12:13], items[13:14], items[14:15],
                items[15:23], items[23:31], items[31:35], items[35:51])
            assert len(items) == 51, len(items)
            out_items = (tab + latq + latkv + swaq[0:4] + kr_ + swaq[4:8] + swak + va_ + mlaq + knope + vmla
                         + gates[0:3] + [stA[0]] + gates[3:6] + [stA[1], stB[0]] + gates[6:8] + [stB[1], stC[0]]
                         + gates[8:12] + [stC[1], stD[0]] + gates[12:16] + [stD[1], spill])

            def wrap(it):
                def w():
                    old = list(pend)
                    del pend[:]
                    it()
                    for f in old:
                        f()
                return w
            return [wrap(it) for it in out_items]

        load_tables_for(0)
        for it in norm_items(0):
            it()
        for c in range(NCH):
            nxt = norm_items(c + 1) if c + 1 < NCH else []
            for it in _interleave(proj_items(c), nxt):
                it()
        S.barrier()
        st.close()

    if 2 in phases:
        st = contextlib.ExitStack()
        KTc = sbt(st, "KTc", [128, 8, SEQ], BF16)
        Vc = sbt(st, "Vc", [128, 32, 4, 192], BF16)
        Wosa = sbt(st, "Wosa", [128, 4, DM], BF16)
        Wosb = sbt(st, "Wosb", [128, 4, DM], BF16)
        Wout = sbt(st, "Wout", [128, 8, DM], BF16)
        QaT_2 = sbt(st, "QaT2", [64, 8, CH], BF16)
        QT_2 = sbt(st, "QT2", [128, 8, CH], BF16)
        Gt = [sbt(st, "Gt%d" % i, [128, 2, CH], BF16) for i in range(3)]
        Ka5 = sbt(st, "Ka5", [64, 2, 5 * 128], BF16)
        Va5 = sbt(st, "Va5", [128, 5, 192], BF16)
        OaT = sbt(st, "OaT2", [128, 4, CH], BF16)
        ObT = sbt(st, "ObT2", [128, 4, CH], BF16)
        yT = sbt(st, "yT", [128, 8, CH], BF16)
        pt = [sbt(st, "pt%d" % i, [128, 2, CH], BF16) for i in range(3)]
        rl = sbt(st, "rl", [128, CH], F32)
        xh = [sbt(st, "xh%d" % i, [128, CH], F32) for i in range(3)]
        ty1 = xh[0]
        mdiag = sbt(st, "mdiag", [128, 4, 128], BF16)
        mprev = sbt(st, "mprev", [128, 4, 128], BF16)
        e128 = sbt(st, "e128", [1, 2, 128], BF16)
        sinkP = sbt(st, "sinkP", [1, 8 * 128], BF16)
        sinkB = sbt(st, "sinkB", [128, 8], F32)
        negc = sbt(st, "negc", [128, 4], F32)
        ty2 = xh[1]

        for (dst, src, k) in ((mdiag, mdiag_d, "mdiag"), (mprev, mprev_d, "mprev"), (e128, e128_d, "e128"),
                              (sinkB, sinkb, "sinkB")):
            S.op("sp", lambda e, dst=dst, src=src: e.dma_start(out=dst[:], in_=src), writes=[k], dma=True)
        S.op("sp", lambda e: e.dma_start(out=xh[0][0:1, :], in_=sinkrep[:, 0:512]), writes=[("xh", 0)], dma=True)
        S.op("sp", lambda e: e.dma_start(out=xh[1][0:1, :], in_=sinkrep[:, 512:1024]), writes=[("xh", 1)], dma=True)
        for g in range(2):
            S.op("pool", lambda e, g=g: e.dma_start(
                out=Wosa[g * 64:(g + 1) * 64, :, :],
                in_=w_osa[g * 256:(g + 1) * 256, :].rearrange("(hh d) n -> d hh n", d=64)),
                writes=["Wosa"], dma=True)
        S.op("pool", lambda e: e.dma_start(out=Wosb[:], in_=w_osb.rearrange("(pr p) n -> p pr n", p=128)),
             writes=["Wosb"], dma=True)
        S.op("pool", lambda e: e.dma_start(out=Wout[:], in_=w_out.rearrange("(kc p) n -> p kc n", p=128)),
             writes=["Wout"], dma=True)

        def nop_(fn, r=(), w=("negc",)):
            S.op("dve", fn, reads=list(r) + ["negc"], writes=list(w))
        nop_(lambda e: e.tensor_reduce(out=negc[:, 2:3], in_=statAq[:], axis=AX.X, op=ALU.max))
        nop_(lambda e: e.tensor_reduce(out=negc[:, 3:4], in_=statAk[:], axis=AX.X, op=ALU.max))
        nop_(lambda e: e.tensor_tensor(out=negc[:, 0:1], in0=negc[:, 2:3], in1=negc[:, 3:4], op=ALU.max))
        nop_(lambda e: e.tensor_reduce(out=negc[:, 2:3], in_=sinkB[:], axis=AX.X, op=ALU.max), r=["sinkB"])
        nop_(lambda e: e.scalar_tensor_tensor(out=negc[:, 0:1], in0=negc[:, 0:1], scalar=SCALE_A,
                                              in1=negc[:, 2:3], op0=ALU.mult, op1=ALU.max))
        nop_(lambda e: e.tensor_scalar(out=negc[:, 0:1], in0=negc[:, 0:1], scalar1=-1.0, scalar2=None,
                                       op0=ALU.mult))
        nop_(lambda e: e.tensor_reduce(out=negc[:, 2:3], in_=statBq[:], axis=AX.X, op=ALU.max))
        nop_(lambda e: e.tensor_reduce(out=negc[:, 3:4], in_=statBk[:], axis=AX.X, op=ALU.max))
        nop_(lambda e: e.tensor_tensor(out=negc[:, 1:2], in0=negc[:, 2:3], in1=negc[:, 3:4], op=ALU.max))
        nop_(lambda e: e.tensor_scalar(out=negc[:, 1:2], in0=negc[:, 1:2], scalar1=-SCALE_B, scalar2=None,
                                       op0=ALU.mult))
        for hf in range(2):
            S.op("act", lambda e, hf=hf: e.activation(out=sinkP[:, hf * 512:(hf + 1) * 512], in_=xh[hf][0:1, :],
                                                      func=AF.Exp, bias=negc[0:1, 0:1]),
                 reads=["negc", ("xh", hf)], writes=["sinkP"])

        brow = sbt(st, "brow", [1, CH], BF16)
        negm = sbt(st, "negm", [128, 1], F32)
        for hh_ in range(2):
            S.op("sp", lambda e, hh_=hh_: e.dma_start(out=KTc[96:97, 4 * hh_:4 * hh_ + 4, :],
                                                      in_=onesrow_d[:, 4 * hh_:4 * hh_ + 4, :]),
                 writes=[("KTc", cc) for cc in range(NCH)], dma=True)
        S.op("dve", lambda e: e.tensor_scalar(out=negm[:], in0=negc[:, 1:2], scalar1=1.0 / SCALE_B, scalar2=None,
                                              op0=ALU.mult),
             reads=["negc"], writes=["negm"])
        S.op("dve", lambda e: e.tensor_scalar(out=brow[:], in0=ones_bf[0:1, 0:1].to_broadcast([1, CH]),
                                              scalar1=negm[0:1, 0:1], scalar2=None, op0=ALU.mult),
             reads=["negm", "ones_bf"], writes=["brow"])
        S.op("sp", lambda e: e.dma_start(out=QT_2[96:97, :, :],
                                         in_=brow[0:1, :].unsqueeze(1).to_broadcast([1, 8, CH])),
             reads=["brow"], writes=["QT_2"], dma=True)

        OB = (6, 7)
        octr = [0]

        def load_chunk(c, q):
            tsl = slice(c * CH, (c + 1) * CH)
            S.op(q, lambda e: e.dma_start(out=QaT_2[:], in_=QaTd[:, :, tsl]),
                 reads=[("QaTd", c)], writes=["QaT_2"], dma=True)
            if c == 0:
                S.op(q, lambda e: e.dma_start(out=Ka5[:, :, 128:640], in_=KaTd[:, :, 0:512]),
                     reads=[("KaTd", 0)], writes=["Ka5"], dma=True)
                S.op(q, lambda e: e.dma_start(out=Va5[:, 1:5, :], in_=Vad[:, 0:4, :, :].rearrange("p b s d -> p b (s d)")),
                     reads=[("Vad", 0)], writes=["Va5"], dma=True)
            else:
                S.op(q, lambda e: e.dma_start(out=Ka5[:], in_=KaTd[:, :, c * CH - 128:(c + 1) * CH]),
                     reads=[("KaTd", c), ("KaTd", c - 1)], writes=["Ka5"], dma=True)
                S.op(q, lambda e: e.dma_start(out=Va5[:], in_=Vad[:, 4 * c - 1:4 * c + 4, :, :].rearrange("p b s d -> p b (s d)")),
                     reads=[("Vad", c), ("Vad", c - 1)], writes=["Va5"], dma=True)
            S.op(q, lambda e: e.dma_start(out=KTc[0:96, :, tsl], in_=KTd[:, :, tsl]),
                 reads=[("KTd", c)], writes=[("KTc", c)], dma=True)
            S.op(q, lambda e: e.dma_start(out=Vc[:, 4 * c:4 * c + 4, :, :],
                                          in_=Vd[:, 4 * c:4 * c + 4, :, :, :].rearrange("p b r s d -> p b r (s d)")),
                 reads=[("Vd", c)], writes=[("Vc", c)], dma=True)
            S.op(q, lambda e: e.dma_start(out=QT_2[0:96, :, :], in_=QTd[:, :, tsl]),
                 reads=[("QTd", c)], writes=["QT_2"], dma=True)

        def load_gt(c, m):
            tsl = slice(c * CH, (c + 1) * CH)
            S.op("sp", lambda e: e.dma_start(out=Gt[m % 3][:, 0, :], in_=Gad[:, m, tsl]),
                 reads=[("Gad", c)], writes=[("Gt", m % 3)], dma=True)
            S.op("sp", lambda e: e.dma_start(out=Gt[m % 3][:, 1, :], in_=Gbd[:, m, tsl]),
                 reads=[("Gbd", c)], writes=[("Gt", m % 3)], dma=True)

        def normalize(ob, o_lo, outT, okey, split4, on_act=False):
            l_lo = 64 - o_lo
            lk = ("rl", l_lo)
            if on_act:
                S.op("act", lambda e: e.activation(out=rl[l_lo:l_lo + 64, :], in_=ps[ob][l_lo:l_lo + 64, :],
                                                   func=AF.Ln),
                     reads=[PS(ob)], writes=[lk])
                S.op("act", lambda e: e.activation(out=rl[l_lo:l_lo + 64, :], in_=rl[l_lo:l_lo + 64, :],
                                                   func=AF.Exp, scale=-1.0),
                     reads=[lk], writes=[lk])
            else:
                S.op("dve", lambda e: e.reciprocal(out=rl[l_lo:l_lo + 64, :], in_=ps[ob][l_lo:l_lo + 64, :]),
                     reads=[PS(ob)], writes=[lk])
            a0 = ps[ob][o_lo:o_lo + 64, :]
            a1 = rl[l_lo:l_lo + 64, :]
            if split4:
                a0 = a0.rearrange("p (h q) -> p h q", h=4)
                a1 = a1.rearrange("p (h q) -> p h q", h=4)
            S.op("dve", lambda e: e.tensor_tensor(out=outT, in0=a0, in1=a1, op=ALU.mult),
                 reads=[PS(ob), lk], writes=[okey])

        def PP(P):
            return [PS(2 * P), PS(2 * P + 1)]

        def swa_jobs(c):
            jobs = []
            for g in range(2):
                for j in range(4):
                    subs = [(j, mprev, "mprev")] if (c > 0 or j > 0) else []
                    subs.append((j + 1, mdiag, "mdiag"))

                    def mk(g=g, j=j, subs=subs):
                        ob = OB[octr[0] % 2]
                        octr[0] += 1
                        qv = QaT_2[0:64, 4 * g:4 * g + 4, j * 128:(j + 1) * 128]
                        ns = len(subs)

                        def qk(P):
                            for si, (kb5, mask, mk_) in enumerate(subs):
                                S.op("pe", lambda e, si=si, kb5=kb5: e.matmul(
                                    pp[P][:, si, :], lhsT=Ka5[:, g, kb5 * 128:(kb5 + 1) * 128], rhs=qv,
                                    start=True, stop=False),
                                    reads=["Ka5", "QaT_2"], writes=[PS(2 * P + si)])
                                S.op("pe", lambda e, si=si, mask=mask: e.matmul(
                                    pp[P][:, si, :], lhsT=ident[:], rhs=mask[:], start=False, stop=True),
                                    reads=["ident", mk_], writes=[PS(2 * P + si)])

                        def ex(P):
                            S.op("act", lambda e: e.activation(out=pt[P][:, 0:ns, :], in_=pp[P][:, 0:ns, :],
                                                               func=AF.Exp, scale=SCALE_A, bias=negc[:, 0:1]),
                                 reads=PP(P)[0:ns] + ["negc"], writes=[("pt", P)])

                        def pv(P):
                            for si, (kb5, mask, mk_) in enumerate(subs):
                                S.op("pe", lambda e, si=si, kb5=kb5: e.matmul(
                                    ps[ob][:, :], lhsT=Va5[:, kb5, g * 64:g * 64 + 128], rhs=pt[P][:, si, :],
                                    start=(si == 0), stop=False),
                                    reads=["Va5", ("pt", P)], writes=[PS(ob)])
                            S.op("pe", lambda e: e.matmul(
                                ps[ob][:, :], lhsT=e128[0:1, g, :], rhs=sinkP[0:1, 4 * g * 128:(4 * g + 4) * 128],
                                start=False, stop=True),
                                reads=["e128", "sinkP"], writes=[PS(ob)])
                            normalize(ob, 64 * g, OaT[64 * g:64 * g + 64, :, j * 128:(j + 1) * 128], "OaT", True,
                                      on_act=(j % 2 == 0))
                        return dict(qk=qk, ex=ex, pv=pv)
                    jobs.append(mk())
            return jobs

        def mla_jobs(c):
            jobs = []
            nk = 4 * c + 4
            KTk = [("KTc", cc) for cc in range(c + 1)]
            Vk = [("Vc", cc) for cc in range(c + 1)]
            for h in range(8):
                ob = OB[octr[0] % 2]
                octr[0] += 1
                pr, par = h // 2, h % 2
                for t in range(nk // 2):
                    kbs = (2 * t, 2 * t + 1)
                    diag = kbs[0] >= 4 * c

                    def mk(h=h, ob=ob, pr=pr, par=par, kbs=kbs, diag=diag):
                        def geom(kb):
                            j = kb - 4 * c
                            q0 = max(j, 0) * 128
                            return q0, CH - q0

                        def qk(P):
                            for si, kb in enumerate(kbs):
                                q0, n = geom(kb)
                                S.op("pe", lambda e, si=si, kb=kb, q0=q0, n=n: e.matmul(
                                    pp[P][:, si, 0:n], lhsT=KTc[0:97, h, kb * 128:(kb + 1) * 128],
                                    rhs=QT_2[0:97, h, q0:CH], start=True, stop=(not diag)),
                                    reads=KTk + ["QT_2"], writes=[PS(2 * P + si)])
                                if diag:
                                    S.op("pe", lambda e, si=si: e.matmul(
                                        pp[P][:, si, 0:128], lhsT=ident[:], rhs=mdiag[:, 0, :],
                                        start=False, stop=True),
                                        reads=["ident", "mdiag"], writes=[PS(2 * P + si)])

                        def ex(P):
                            if not diag:
                                S.op("act", lambda e: e.activation(out=pt[P][:], in_=pp[P][:], func=AF.Exp,
                                                                   scale=SCALE_B),
                                     reads=PP(P), writes=[("pt", P)])
                            else:
                                for si, kb in enumerate(kbs):
                                    q0, n = geom(kb)
                                    S.op("act", lambda e, si=si, n=n: e.activation(
                                        out=pt[P][:, si, 0:n], in_=pp[P][:, si, 0:n], func=AF.Exp,
                                        scale=SCALE_B),
                                        reads=[PS(2 * P + si)], writes=[("pt", P)])

                        def pv(P):
                            for si, kb in enumerate(kbs):
                                q0, n = geom(kb)
                                last = (kb == nk - 1)
                                S.op("pe", lambda e, si=si, kb=kb, q0=q0, n=n, last=last: e.matmul(
                                    ps[ob][:, q0:CH], lhsT=Vc[:, kb, pr, par * 64:par * 64 + 128],
                                    rhs=pt[P][:, si, 0:n], start=(kb == 0), stop=last),
                                    reads=Vk + [("pt", P)], writes=[PS(ob)])
                            if kbs[1] == nk - 1:
                                normalize(ob, 64 * par, ObT[64 * par:64 * par + 64, pr, :], "ObT", False,
                                          on_act=(c <= 1))
                        return dict(qk=qk, ex=ex, pv=pv)
                    jobs.append(mk())
            return jobs

        def outproj_jobs(c):
            jobs = []
            for j in range(4):
                for n_ in range(2):
                    def mk(j=j, n_=n_):
                        s_ = 2 * j + n_
                        hb = s_ % 3
                        t0 = c * CH + j * 128
                        nsl = slice(n_ * 512, (n_ + 1) * 512)

                        def qk(P):
                            S.op("sp", lambda e: e.dma_start(out=xh[hb][:], in_=x[t0:t0 + 128, nsl]),
                                 writes=[("xh", hb)], dma=True)
                            mm_group(pp[P][:, 0, :], [(yT[:, kc, j * 128:(j + 1) * 128], Wout[:, kc, nsl])
                                                      for kc in range(8)], ["yT", "Wout"], 2 * P)

                        def ex(P):
                            S.op("dve", lambda e: e.tensor_tensor(out=xh[hb][:], in0=pp[P][:, 0, :], in1=xh[hb][:],
                                                                  op=ALU.add),
                                 reads=[PS(2 * P), ("xh", hb)], writes=[("xh", hb)])
                            S.op("pool", lambda e: e.dma_start(out=x1d[t0:t0 + 128, nsl], in_=xh[hb][:]),
                                 reads=[("xh", hb)], writes=[("x1d", c, j, n_)], dma=True)

                        def pv(P):
                            pass
                        return dict(qk=qk, ex=ex, pv=pv)
                    jobs.append(mk())
            return jobs

        LOOK = 2

        def run_jobs(jobs, pctr):
            n = len(jobs)
            base = pctr[0]
            for k in range(n + LOOK):
                if k < n:
                    jobs[k]["qk"]((base + k) % 3)
                if k - LOOK >= 0:
                    P = (base + k - LOOK) % 3
                    jobs[k - LOOK]["ex"](P)
                    jobs[k - LOOK]["pv"](P)
            pctr[0] = base + n

        pctr = [0]
        load_chunk(0, "sp")
        for m in range(3):
            load_gt(0, m)

        def chunk2(c):
            mj = mla_jobs(c)
            if c > 0:
                mj = _interleave(mj, outproj_jobs(c - 1))
            run_jobs(swa_jobs(c) + mj, pctr)
            if c + 1 < NCH:
                load_chunk(c + 1, "pool")

            for m in range(8):
                P = m % 3
                msl = slice(m * 128, (m + 1) * 128)
                mm_group(pp[P][:, 0, :], [(Wosa[:, hh, msl], OaT[:, hh, :]) for hh in range(4)],
                         ["Wosa", "OaT"], 2 * P)
                mm_group(pp[P][:, 1, :], [(Wosb[:, pr, msl], ObT[:, pr, :]) for pr in range(4)],
                         ["Wosb", "ObT"], 2 * P + 1)
                S.op("dve", lambda e, P=P, m=m: e.tensor_tensor(out=ty1[:], in0=pp[P][:, 0, :], in1=Gt[m % 3][:, 0, :],
                                                                op=ALU.mult),
                     reads=[PS(2 * P), ("Gt", m % 3)], writes=[("xh", 0)])
                S.op("dve", lambda e, P=P, m=m: e.tensor_tensor(out=ty2[:], in0=pp[P][:, 1, :], in1=Gt[m % 3][:, 1, :],
                                                                op=ALU.mult),
                     reads=[PS(2 * P + 1), ("Gt", m % 3)], writes=[("xh", 1)])
                S.op("dve", lambda e, m=m: e.tensor_tensor(out=yT[:, m, :], in0=ty1[:], in1=ty2[:], op=ALU.add),
                     reads=[("xh", 0), ("xh", 1)], writes=["yT"])
                if m + 3 < 8:
                    load_gt(c, m + 3)
            if c + 1 < NCH:
                for m in range(3):
                    load_gt(c + 1, m)

            if debug and c == 0:
                S.op("sp", lambda e: e.dma_start(out=dOaT, in_=OaT[:]), reads=["OaT"], dma=True)
                S.op("sp", lambda e: e.dma_start(out=dObT, in_=ObT[:]), reads=["ObT"], dma=True)
                S.op("sp", lambda e: e.dma_start(out=dyT, in_=yT[:]), reads=["yT"], dma=True)
                S.op("sp", lambda e: e.dma_start(out=dVc, in_=Vc[:, 0:4, :, :]), reads=[("Vc", 0)], dma=True)
                S.op("sp", lambda e: e.dma_start(out=dKT, in_=KTc[:, :, 0:CH]), reads=[("KTc", 0)], dma=True)
                S.op("sp", lambda e: e.dma_start(out=dQT, in_=QT_2[:]), reads=["QT_2"], dma=True)
            if c == NCH - 1:
                run_jobs(outproj_jobs(c), pctr)

        for c in range(NCH):
            chunk2(c)
        S.barrier()
        st.close()

    if 3 in phases:
        st = contextlib.ExitStack()
        Wg = sbt(st, "Wg", [128, 8, DFF], BF16)
        Wu = sbt(st, "Wu", [128, 8, DFF], BF16)
        Wd = sbt(st, "Wd", [128, NFF, DM], BF16)
        gffn = sbt(st, "gffn", [128, 8], F32)
        gfin = sbt(st, "gfin", [128, DM], F32)
        xb = [sbt(st, "fxb%d" % i, [128, DM], F32) for i in range(2)]
        junk = sbt(st, "fjunk", [128, DM], BF16)
        xs = [sbt(st, "fxs%d" % i, [128, DM], BF16) for i in range(2)]
        hT = [sbt(st, "fhT%d" % i, [128, 8, CH], BF16) for i in range(2)]
        actT = sbt(st, "actT", [128, NFF, CH], BF16)
        sg = [sbt(st, "sg%d" % i, [128, CH], F32) for i in range(2)]
        xr = sbt(st, "xr", [128, DM], F32)
        x2 = sbt(st, "x2", [128, DM], F32)
        ob_ = sbt(st, "ob_", [128, DM], F32)
        junk2 = sbt(st, "junk2", [128, DM], BF16)
        ss2 = sbt(st, "ss2", [128, 1], F32)
        rs2 = sbt(st, "rs2", [128, 1], F32)

        FG = ((0, 768), (768, 1536), (1536, 2176), (2176, DFF))

        def FGRP(m):
            return [gi for gi, (a, b_) in enumerate(FG) if a <= m * 128 < b_][0]
        for gi, (a, b_) in enumerate(FG):
            S.op("pool", lambda e, a=a, b_=b_: e.dma_start(
                out=Wg[:, :, a:b_], in_=w_gate[:, a:b_].rearrange("(kc p) n -> p kc n", p=128)),
                writes=[("Wgu", gi)], dma=True)
            S.op("pool", lambda e, a=a, b_=b_: e.dma_start(
                out=Wu[:, :, a:b_], in_=w_up[:, a:b_].rearrange("(kc p) n -> p kc n", p=128)),
                writes=[("Wgu", gi)], dma=True)
        for (m0, m1) in ((0, 6), (6, 12), (12, 17), (17, NFF)):
            S.op("pool", lambda e, m0=m0, m1=m1: e.dma_start(
                out=Wd[:, m0:m1, :], in_=w_down[m0 * 128:m1 * 128, :].rearrange("(m p) n -> p m n", p=128)),
                writes=["Wd"], dma=True)
        S.op("sp", lambda e: e.dma_start(out=gffn[:], in_=ffn_g.rearrange("(kc p) -> p kc", p=128),
                                         allow_slow_non_contiguous=True), writes=["gT"], dma=True)
        S.op("sp", lambda e: e.dma_start(out=gfin[:], in_=fin_g.partition_broadcast(128)),
             writes=["gfin"], dma=True)

        def ffn_norm_items(c):
            return norm_sched([norm_block(x1d, c * CH + j * 128, xb, junk, xs,
                                          hT[c % 2][:, :, j * 128:(j + 1) * 128], ("fhT", c % 2, j), gffn)
                               for j in range(4)])

        def ffn_items(c):
            items = []
            hK = [("fhT", c % 2, j) for j in range(4)]
            h_ = hT[c % 2]
            for m in range(NFF):
                def it(m=m):
                    bg, bu = nb(), nb()
                    msl = slice(m * 128, (m + 1) * 128)
                    wk = [("Wgu", FGRP(m))]
                    mm_group(ps[bg][:, :], [(Wg[:, kc, msl], h_[:, kc, :]) for kc in range(8)], hK + wk, bg)
                    mm_group(ps[bu][:, :], [(Wu[:, kc, msl], h_[:, kc, :]) for kc in range(8)], hK + wk, bu)
                    s_ = m % 2
                    S.op("act", lambda e: e.activation(out=sg[s_][:], in_=ps[bg][:, :], func=AF.Silu),
                         reads=[PS(bg)], writes=[("sg", s_)])
                    S.op("dve", lambda e: e.tensor_tensor(out=actT[:, m, :], in0=ps[bu][:, :],
                                                          in1=sg[s_][:], op=ALU.mult),
                         reads=[PS(bu), ("sg", s_)], writes=[("actT", m)])
                items.append(it)
            aK = [("actT", m) for m in range(NFF)]
            for j in range(4):
                def it(j=j):
                    t0 = c * CH + j * 128
                    S.op("sp", lambda e: e.dma_start(out=xr[:], in_=x1d[t0:t0 + 128, :]),
                         reads=[("x1d", c, j, 0), ("x1d", c, j, 1)], writes=["xr"], dma=True)
                    for n_ in range(2):
                        bo = nb()
                        mm_group(ps[bo][:, :], [(actT[:, m, j * 128:(j + 1) * 128], Wd[:, m, n_ * 512:(n_ + 1) * 512])
                                                for m in range(NFF)], aK + ["Wd"], bo)
                        S.op("dve", lambda e, bo=bo, n_=n_: e.tensor_tensor(
                            out=x2[:, n_ * 512:(n_ + 1) * 512], in0=ps[bo][:, :], in1=xr[:, n_ * 512:(n_ + 1) * 512],
                            op=ALU.add),
                            reads=[PS(bo), "xr"], writes=["x2"])
                    S.op("act", lambda e: e.activation(out=junk2[:], in_=x2[:], func=AF.Square, accum_out=ss2[:]),
                         reads=["x2"], writes=["junk2", "ss2"])
                    S.op("act", lambda e: e.activation(out=rs2[:], in_=ss2[:], func=AF.Ln, scale=1.0 / DM, bias=EPS),
                         reads=["ss2"], writes=["rs2"])
                    S.op("act", lambda e: e.activation(out=rs2[:], in_=rs2[:], func=AF.Exp, scale=-0.5),
                         reads=["rs2"], writes=["rs2"])
                    S.op("dve", lambda e: e.scalar_tensor_tensor(out=ob_[:], in0=x2[:], scalar=rs2[:, 0:1], in1=gfin[:],
                                                                 op0=ALU.mult, op1=ALU.mult),
                         reads=["x2", "rs2", "gfin"], writes=["ob_"])
                    S.op("sp", lambda e: e.dma_start(out=out[t0:t0 + 128, :], in_=ob_[:]),
                         reads=["ob_"], dma=True)
                items.append(it)
            return items

        for it in ffn_norm_items(0):
            it()
        for c in range(NCH):
            nxt = ffn_norm_items(c + 1) if c + 1 < NCH else []
            for it in _interleave(ffn_items(c), nxt):
                it()
        st.close()

    S.emit()
    top.close()
    return nc


def _consts():
    bf = ml_dtypes.bfloat16
    pos = np.arange(SEQ, dtype=np.float32)[:, None]

    def tables(dim):
        inv = (10000.0 ** (-np.arange(0, dim, 2, dtype=np.float32) / dim)).astype(np.float32)
        ang = (pos * inv[None, :]).astype(np.float32)
        c = np.cos(ang).astype(np.float32).T
        s = np.sin(ang).astype(np.float32).T
        return (np.ascontiguousarray(np.concatenate([c, c], 0)),
                np.ascontiguousarray(np.concatenate([-s, s], 0)))

    cosA, sinA = tables(64)
    cosB, sinB = tables(32)
    k = np.arange(128)[:, None]
    q = np.arange(128)[None, :]
    mdiag = np.where(k <= q, 0.0, NEG).astype(np.float32)
    mprev = np.where(k > q, 0.0, NEG).astype(np.float32)
    e128 = np.zeros((1, 2, 128), np.float32)
    e128[0, 0, 64:] = 1.0
    e128[0, 1, :64] = 1.0
    return {
        "cosA": cosA, "sinA": sinA, "cosB": cosB, "sinB": sinB,
        "ident": np.eye(128, dtype=np.float32).astype(bf),
        "mdiag4": np.ascontiguousarray(np.broadcast_to(mdiag[:, None, :], (128, 4, 128))).astype(bf),
        "mprev4": np.ascontiguousarray(np.broadcast_to(mprev[:, None, :], (128, 4, 128))).astype(bf),
        "e128": e128.astype(bf),
        "ropeperm": _ropeperm().astype(bf),
        "onesrow": np.ones((1, 8, SEQ), np.float32).astype(bf),
    }


def _ropeperm():
    p = np.zeros((128, 64), np.float32)
    for m in range(64):
        p[(m + 32) % 64, m] = 1.0
    for m in range(32):
        p[64 + (m + 16) % 32, m] = 1.0
    return p


_NC_CACHE = {}


def kernel(x, mix_norm_g, w_in, swa_sinks, q_norm_g, w_uq, kv_norm_g, w_ukv,
           w_o_swa, w_o_mla, w_out, ffn_norm_g, w_gate, w_up, w_down, final_norm_g,
           _debug=False, _phases=(1, 2, 3)):
    f = lambda a: np.ascontiguousarray(np.asarray(a, dtype=np.float32))
    x = f(x)
    sinks = f(swa_sinks)[0]
    shared = {
        "mix_norm_g": f(mix_norm_g)[0], "w_in": f(w_in)[0], "q_norm_g": f(q_norm_g)[0],
        "w_uq": f(w_uq)[0], "kv_norm_g": f(kv_norm_g)[0], "w_ukv": f(w_ukv)[0],
        "w_o_swa": f(w_o_swa)[0], "w_o_mla": f(w_o_mla)[0], "w_out": f(w_out)[0],
        "ffn_norm_g": f(ffn_norm_g)[0], "w_gate": f(w_gate)[0], "w_up": f(w_up)[0],
        "w_down": f(w_down)[0], "final_norm_g": f(final_norm_g),
        "sinkrep": np.ascontiguousarray(np.repeat(sinks, 128)[None, :]),
        "sinkb": np.ascontiguousarray(np.broadcast_to(sinks[None, :], (128, 8))),
    }
    shared.update(_consts())
    nc = build_nc(debug=_debug, phases=_phases)
    in_maps = []
    for b in range(8):
        m = dict(shared)
        m["x"] = np.ascontiguousarray(x[b])
        in_maps.append(m)
    res = run_bass_kernel_spmd(nc, in_maps, core_ids=list(range(8)))
    if _debug:
        return res.results
    return np.stack([np.asarray(r["out"], dtype=np.float32) for r in res.results], axis=0)
```

```python
import contextlib
import numpy as np
import ml_dtypes
import concourse.bass as bass
import concourse.mybir as mybir
from concourse.bass_utils import run_bass_kernel_spmd

F32, BF16 = mybir.dt.float32, mybir.dt.bfloat16
AF = mybir.ActivationFunctionType
ALU = mybir.AluOpType
AX = mybir.AxisListType

SEQ = 4096
DM = 1024
import os
NCH = int(os.environ.get('K_NCH', '8'))
CH = 512
DFF = 2816
NFF = 22
SCALE_A = 64 ** -0.5
SCALE_B = 96 ** -0.5
NEG = -30000.0
EPS = 1e-6

ENGS = ("pe", "act", "dve", "pool", "sp")
N_DMA_SEMS = 24


class _Op:
    __slots__ = ("eng", "fn", "deps", "is_dma", "signal", "seq", "dsem", "dval", "pre_wait")

    def __init__(self, eng, fn, is_dma):
        self.eng = eng
        self.fn = fn
        self.is_dma = is_dma
        self.deps = []
        self.signal = False
        self.seq = None
        self.dsem = None
        self.dval = None
        self.pre_wait = None


class Sched:
    def __init__(self, nc):
        self.nc = nc
        self.ops = {e: [] for e in ENGS}
        self.all = []
        self.last_w = {}
        self.readers = {}
        self.dma_i = 0
        self.dma_ip = 0
        self.dma_last = [None] * N_DMA_SEMS
        self.dma_val = [0] * N_DMA_SEMS
        self.last_eng = {e: None for e in ENGS}
        self.pending_barrier = {e: [] for e in ENGS}

    def barrier(self):
        deps = [o for o in self.last_eng.values() if o is not None]
        deps += [o for o in self.dma_last if o is not None]
        for d in deps:
            d.signal = True
        for e in ENGS:
            self.pending_barrier[e] = list(deps)

    def op(self, eng, fn, reads=(), writes=(), dma=False):
        o = _Op(eng, fn, dma)
        psr = [k for k in reads if isinstance(k, tuple) and k[0] in ("ps", "pT")]
        if psr:
            reads = [k for k in reads if k not in psr]
            writes = list(writes) + psr
        deps = set()
        for k in reads:
            w = self.last_w.get(k)
            if w is not None:
                deps.add(w)
        for k in writes:
            w = self.last_w.get(k)
            if w is not None:
                deps.add(w)
            for r in self.readers.get(k, ()):
                deps.add(r)
        if self.pending_barrier[eng]:
            deps.update(self.pending_barrier[eng])
            self.pending_barrier[eng] = []
        for d in deps:
            if d.eng == "pe" and eng == "pe" and not d.is_dma and not dma:
                continue
            o.deps.append(d)
            d.signal = True
        for k in writes:
            self.last_w[k] = o
            self.readers[k] = []
        for k in reads:
            if k in writes:
                continue
            self.readers.setdefault(k, []).append(o)
        if dma:
            if eng == "pool":
                s = 16 + self.dma_ip % 8
                self.dma_ip += 1
            else:
                s = self.dma_i % 16
                self.dma_i += 1
            o.pre_wait = self.dma_last[s]
            self.dma_val[s] += 16
            o.dsem = s
            o.dval = self.dma_val[s]
            self.dma_last[s] = o
        else:
            self.last_eng[eng] = o
        self.ops[eng].append(o)
        self.all.append(o)
        return o

    def emit(self):
        nc = self.nc
        cnt = {e: 0 for e in ENGS}
        for o in self.all:
            if (not o.is_dma) and o.signal:
                cnt[o.eng] += 1
                o.seq = cnt[o.eng]
        dma_val = self.dma_val
        with contextlib.ExitStack() as st:
            esem = {e: st.enter_context(nc.semaphore("s_" + e)) for e in ENGS}
            dsem = [st.enter_context(nc.semaphore("d%d" % i)) for i in range(N_DMA_SEMS)]
            block = st.enter_context(nc.Block())

            def run(engname, eng):
                waited = {}

                def wait(key, sem, val):
                    if waited.get(key, 0) >= val:
                        return
                    waited[key] = val
                    eng.wait_ge(sem, val)

                for o in self.ops[engname]:
                    for d in o.deps:
                        if d.is_dma:
                            wait(("d", d.dsem), dsem[d.dsem], d.dval)
                        else:
                            wait(("e", d.eng), esem[d.eng], d.seq)
                    if o.is_dma:
                        p = o.pre_wait
                        if p is not None:
                            wait(("d", p.dsem), dsem[p.dsem], p.dval)
                        ins = o.fn(eng)
                        ins.then_inc(dsem[o.dsem], 16)
                    else:
                        ins = o.fn(eng)
                        if o.signal:
                            ins.then_inc(esem[engname], 1)
                if engname == "sp":
                    for s in range(N_DMA_SEMS):
                        if dma_val[s] > 0:
                            wait(("d", s), dsem[s], dma_val[s])

            @block.tensor
            def _(e):
                run("pe", e)

            @block.scalar
            def _(e):
                run("act", e)

            @block.vector
            def _(e):
                run("dve", e)

            @block.gpsimd
            def _(e):
                run("pool", e)

            @block.sync
            def _(e):
                run("sp", e)


def _interleave(a, b):
    if not b:
        return list(a)
    out = []
    step = max(1, len(a) // (len(b) + 1))
    bi = 0
    for i, it in enumerate(a):
        out.append(it)
        if bi < len(b) and (i + 1) % step == 0:
            out.append(b[bi])
            bi += 1
    out.extend(b[bi:])
    return out


def build_nc(debug=False, phases=(1, 2, 3)):
    nc = bass.Bass("TRN2", target_bir_lowering=False)

    def din(name, shape, dt=F32):
        return nc.dram_tensor(name, list(shape), dt, kind="ExternalInput").ap()

    x = din("x", [SEQ, DM])
    mix_g = din("mix_norm_g", [DM])
    w_in = din("w_in", [DM, 3488])
    q_g = din("q_norm_g", [384])
    w_uq = din("w_uq", [384, 768])
    kv_g = din("kv_norm_g", [256])
    w_ukv = din("w_ukv", [256, 1024])
    w_osa = din("w_o_swa", [512, DM])
    w_osb = din("w_o_mla", [512, DM])
    w_out = din("w_out", [DM, DM])
    ffn_g = din("ffn_norm_g", [DM])
    w_gate = din("w_gate", [DM, DFF])
    w_up = din("w_up", [DM, DFF])
    w_down = din("w_down", [DFF, DM])
    fin_g = din("final_norm_g", [DM])
    sinkrep = din("sinkrep", [1, 8 * 128])
    sinkb = din("sinkb", [128, 8])
    cosA = din("cosA", [64, SEQ])
    sinA = din("sinA", [64, SEQ])
    cosB = din("cosB", [32, SEQ])
    sinB = din("sinB", [32, SEQ])
    ident_d = din("ident", [128, 128], BF16)
    mdiag_d = din("mdiag4", [128, 4, 128], BF16)
    mprev_d = din("mprev4", [128, 4, 128], BF16)
    e128_d = din("e128", [1, 2, 128], BF16)
    perm_d = din("ropeperm", [128, 64], BF16)
    onesrow_d = din("onesrow", [1, 8, SEQ], BF16)

    out = nc.dram_tensor("out", [SEQ, DM], F32, kind="ExternalOutput").ap()

    skind = "ExternalOutput" if debug else "Internal"

    def dscr(name, shape, dt):
        return nc.dram_tensor(name, list(shape), dt, kind=skind).ap()

    QaTd = dscr("QaTd", [64, 8, SEQ], BF16)
    KaTd = dscr("KaTd", [64, 2, SEQ], BF16)
    Vad = dscr("Vad", [128, 32, 3, 64], BF16)
    QTd = dscr("QTd", [96, 8, SEQ], BF16)
    KTd = dscr("KTd", [96, 8, SEQ], BF16)
    Vd = dscr("Vd", [128, 32, 4, 3, 64], BF16)
    Gad = dscr("Gad", [128, 8, SEQ], BF16)
    Gbd = dscr("Gbd", [128, 8, SEQ], BF16)
    x1d = dscr("x1d", [SEQ, DM], F32)
    if debug:
        dOaT = dscr("dOaT", [128, 4, CH], BF16)
        dObT = dscr("dObT", [128, 4, CH], BF16)
        dyT = dscr("dyT", [128, 8, CH], BF16)
        dVc = dscr("dVc", [128, 4, 4, 192], BF16)
        dKT = dscr("dKT", [128, 8, CH], BF16)
        dQT = dscr("dQT", [128, 8, CH], BF16)

    S = Sched(nc)
    top = contextlib.ExitStack()

    def sbt(stack, name, shape, dt):
        return stack.enter_context(nc.sbuf_tensor("sb_" + name, list(shape), dt))

    pp = [top.enter_context(nc.psum_tensor("pp%d" % i, [128, 2, 512], F32)) for i in range(3)]
    ps = [pp[i // 2][:, i % 2, :] for i in range(6)]
    ps += [top.enter_context(nc.psum_tensor("ps%d" % i, [128, 512], F32))[:] for i in (6, 7)]
    pT = [ps[6 + i].bitcast(BF16).rearrange("p (k c) -> p k c", k=8) for i in range(2)]

    ident = sbt(top, "ident", [128, 128], BF16)
    ones_bf = sbt(top, "ones_bf", [128, 128], BF16)
    statAq = sbt(top, "statAq", [128, 64], F32)
    statAk = sbt(top, "statAk", [128, 16], F32)
    statBq = sbt(top, "statBq", [128, 64], F32)
    statBk = sbt(top, "statBk", [128, 64], F32)
    ss = sbt(top, "ss", [128, 2], F32)
    rstd = sbt(top, "rstd", [128, 2], F32)

    S.op("sp", lambda e: e.dma_start(out=ident[:], in_=ident_d), writes=["ident"], dma=True)
    S.op("pool", lambda e: e.memset(ones_bf[:], 1.0), writes=["ones_bf"])
    for stt in (statAq, statAk, statBq, statBk):
        S.op("pool", lambda e, stt=stt: e.memset(stt[:], 0.0), writes=[("statinit", id(stt))])

    bank_ctr = [0]

    def nb():
        b = bank_ctr[0] % 6
        bank_ctr[0] += 1
        return b

    def PS(b):
        return ("ps", b)

    blk_ctr = [0]

    def norm_block(src, t0, xb, junk, xs, hT_ap, hT_key, gT):
        b = blk_ctr[0] % 2
        blk_ctr[0] += 1

        def prep():
            S.op("sp", lambda e: e.dma_start(out=xb[b][:], in_=src[t0:t0 + 128, :]),
                 writes=[("xb", b)], dma=True)
            S.op("act", lambda e: e.activation(out=junk[:], in_=xb[b][:], func=AF.Square,
                                               accum_out=ss[:, b:b + 1]),
                 reads=[("xb", b)], writes=["junk", ("ss", b)])
            S.op("act", lambda e: e.activation(out=rstd[:, b:b + 1], in_=ss[:, b:b + 1], func=AF.Ln,
                                               scale=1.0 / DM, bias=EPS),
                 reads=[("ss", b)], writes=[("rstd", b)])
            S.op("act", lambda e: e.activation(out=rstd[:, b:b + 1], in_=rstd[:, b:b + 1], func=AF.Exp,
                                               scale=-0.5),
                 reads=[("rstd", b)], writes=[("rstd", b)])
            S.op("dve", lambda e: e.tensor_scalar(out=xs[b][:], in0=xb[b][:], scalar1=rstd[:, b:b + 1],
                                                  scalar2=None, op0=ALU.mult),
                 reads=[("xb", b), ("rstd", b)], writes=[("xs", b)])

        def xpose():
            for kc in range(8):
                S.op("pe", lambda e, kc=kc: e.transpose(out=pT[b][:, kc, :],
                                                        in_=xs[b][:, kc * 128:(kc + 1) * 128],
                                                        identity=ident[:]),
                     reads=[("xs", b), "ident"], writes=[PS(6 + b)])
            S.op("dve", lambda e: e.tensor_tensor(out=hT_ap, in0=pT[b],
                                                  in1=gT[:].unsqueeze(2).to_broadcast([128, 8, 128]),
                                                  op=ALU.mult),
                 reads=[PS(6 + b), "gT"], writes=[hT_key])
        return prep, xpose

    def norm_sched(blocks):
        p = [b_[0] for b_ in blocks]
        x_ = [b_[1] for b_ in blocks]
        return [p[0], p[1], x_[0], p[2], x_[1], p[3], x_[2], x_[3]]

    def mm_group(out_ap, pairs, reads, bank, extra=()):
        n = len(pairs) + len(extra)
        i = 0
        for (l, r) in list(pairs) + list(extra):
            S.op("pe", lambda e, l=l, r=r, i=i: e.matmul(out_ap, lhsT=l, rhs=r,
                                                         start=(i == 0), stop=(i == n - 1)),
                 reads=reads, writes=[PS(bank)])
            i += 1

    if 1 in phases:
        st = contextlib.ExitStack()
        Win = sbt(st, "Win", [128, 8, 3488], BF16)
        Wsw = sbt(st, "Wsw", [128, 8, 672], BF16)
        Wuq = sbt(st, "Wuq", [128, 3, 768], BF16)
        Wuqsw = sbt(st, "Wuqsw", [128, 3, 256], BF16)
        Wukv = sbt(st, "Wukv", [128, 2, 1024], BF16)
        gmix = sbt(st, "gmix", [128, 8], F32)
        gq = sbt(st, "gq", [128, 3], F32)
        gkv = sbt(st, "gkv", [128, 2], F32)
        xb = [sbt(st, "xb%d" % i, [128, DM], F32) for i in range(2)]
        junk = sbt(st, "junk", [128, DM], BF16)
        xs = [sbt(st, "xs%d" % i, [128, DM], BF16) for i in range(2)]
        hT = [sbt(st, "hT%d" % i, [128, 8, CH], BF16) for i in range(2)]
        cA2 = [sbt(st, "cA%d" % i, [64, CH], F32) for i in range(2)]
        sA2 = [sbt(st, "sA%d" % i, [64, CH], F32) for i in range(2)]
        cB2 = [sbt(st, "cB%d" % i, [128, CH], F32) for i in range(2)]
        sB2 = [sbt(st, "sB%d" % i, [128, CH], F32) for i in range(2)]

        def load_tables_for(cc):
            tsl_ = slice(cc * CH, (cc + 1) * CH)
            i_ = cc % 2
            S.op("sp", lambda e: e.dma_start(out=cA2[i_][:], in_=cosA[:, tsl_]), writes=[("cA", i_)], dma=True)
            S.op("sp", lambda e: e.dma_start(out=sA2[i_][:], in_=sinA[:, tsl_]), writes=[("sA", i_)], dma=True)
            S.op("sp", lambda e: e.dma_start(out=cB2[i_][64:96, :], in_=cosB[:, tsl_]), writes=[("cB", i_)], dma=True)
            S.op("sp", lambda e: e.dma_start(out=sB2[i_][64:96, :], in_=sinB[:, tsl_]), writes=[("sB", i_)], dma=True)
        QaT = sbt(st, "QaT", [64, 8, CH], BF16)
        KaT = sbt(st, "KaT", [64, 2, CH], BF16)
        Va = sbt(st, "Va", [128, 4, 3, 64], BF16)
        qlf = sbt(st, "qlf", [128, 3, CH], F32)
        sq = sbt(st, "sq", [128, 3, CH], BF16)
        rq = sbt(st, "rq", [128, CH], F32)
        cqT = sbt(st, "cqT", [128, 3, CH], BF16)
        ckvT = sbt(st, "ckvT", [128, 2, CH], BF16)
        QT = sbt(st, "QT", [96, 8, CH], BF16)
        KT = sbt(st, "KT", [96, 8, CH], BF16)
        Vb = sbt(st, "Vb", [128, 4, 4, 3, 64], BF16)
        Ga = sbt(st, "Ga", [128, 8, CH], BF16)
        Gb = sbt(st, "Gb", [128, 8, CH], BF16)
        t1 = sbt(st, "t1", [128, CH], F32)
        t2 = sbt(st, "t2", [128, CH], F32)
        sq8 = sbt(st, "sq8", [128, 8, CH], BF16)
        Pm = sbt(st, "Pm", [128, 64], BF16)
        qbf = [sbt(st, "qbf%d" % i, [128, CH], BF16) for i in range(2)]
        S.op("sp", lambda e: e.dma_start(out=Pm[:], in_=perm_d), writes=["Pm"], dma=True)
        qctr = [0]
        pend = []

        WGRP = ((0, 768), (768, 1440), (1440, 2464), (2464, 3488))

        def WK(lo):
            return [("Win", gi) for gi, (a, b_) in enumerate(WGRP) if a <= lo < b_]
        for gi, (a, b_) in enumerate(WGRP):
            S.op("pool", lambda e, a=a, b_=b_: e.dma_start(
                out=Win[:, :, a:b_], in_=w_in[:, a:b_].rearrange("(kc p) n -> p kc n", p=128)),
                writes=[("Win", gi)], dma=True)
        S.op("pool", lambda e: e.dma_start(out=Wuq[:], in_=w_uq.rearrange("(i p) n -> p i n", p=128)),
             writes=["Wuq"], dma=True)
        S.op("pool", lambda e: e.dma_start(out=Wukv[:], in_=w_ukv.rearrange("(i p) n -> p i n", p=128)),
             writes=["Wukv"], dma=True)
        S.op("sp", lambda e: e.dma_start(out=gmix[:], in_=mix_g.rearrange("(kc p) -> p kc", p=128),
                                         allow_slow_non_contiguous=True), writes=["gT"], dma=True)
        S.op("sp", lambda e: e.dma_start(out=gq[:], in_=q_g.rearrange("(kc p) -> p kc", p=128),
                                         allow_slow_non_contiguous=True), writes=["gq"], dma=True)
        S.op("sp", lambda e: e.dma_start(out=gkv[:], in_=kv_g.rearrange("(kc p) -> p kc", p=128),
                                         allow_slow_non_contiguous=True), writes=["gkv"], dma=True)
        WinK = [("Win", 0), ("Win", 1)]
        for (src_lo, n_h, half, dst_lo) in ((0, 8, 32, 0), (512, 2, 32, 512), (1408, 1, 16, 640)):
            w = n_h * 2 * half
            sv = Win[:, :, src_lo:src_lo + w].rearrange("p k (h t d) -> p k h t d", t=2, d=half)
            dv = Wsw[:, :, dst_lo:dst_lo + w].rearrange("p k (h t d) -> p k h t d", t=2, d=half)
            for t in range(2):
                S.op("pool", lambda e, sv=sv, dv=dv, t=t: e.tensor_copy(out=dv[:, :, :, t, :],
                                                                         in_=sv[:, :, :, 1 - t, :]),
                     reads=WinK, writes=["Wsw"])
        sv = Wuq[:].rearrange("p k (h f) -> p k h f", f=96)[:, :, :, 64:96].rearrange(
            "p k h (t d) -> p k h t d", t=2)
        dv = Wuqsw[:].rearrange("p k (h t d) -> p k h t d", t=2, d=16)
        for t in range(2):
            S.op("pool", lambda e, sv=sv, dv=dv, t=t: e.tensor_copy(out=dv[:, :, :, t, :],
                                                                     in_=sv[:, :, :, 1 - t, :]),
                 reads=["Wuq"], writes=["Wuqsw"])
        S.op("dve", lambda e: e.memset(sq8[:], 0.0), writes=["sq8"])
        S.op("dve", lambda e: e.memset(Va[:, :, 1, :], 1.0), writes=["Va"])
        for j in range(4):
            S.op("dve", lambda e, j=j: e.memset(Vb[:, j, :, 1, :], 1.0), writes=["Vb"])

        def norm_items(c):
            return norm_sched([norm_block(x, c * CH + j * 128, xb, junk, xs,
                                          hT[c % 2][:, :, j * 128:(j + 1) * 128], ("hT", c % 2, j), gmix)
                               for j in range(4)])

        def proj_items(c):
            items = []
            h_ = hT[c % 2]
            hK = [("hT", c % 2, j) for j in range(4)]
            tsl = slice(c * CH, (c + 1) * CH)

            def w_pairs(W, lo, hi):
                return [(W[:, kc, lo:hi], h_[:, kc, :]) for kc in range(8)]

            cA, sA, cB, sB = cA2[c % 2], sA2[c % 2], cB2[c % 2], sB2[c % 2]
            kcA, ksA, kcB, ksB = ("cA", c % 2), ("sA", c % 2), ("cB", c % 2), ("sB", c % 2)

            def load_tables():
                if c + 1 < NCH:
                    load_tables_for(c + 1)
            items.append(load_tables)

            def rope_evac(b0, b1, prt, ctab, stab, ckey, skey, out_ap, out_key):
                S.op("dve", lambda e: e.tensor_tensor(out=t1[prt, :], in0=ps[b0][prt, :], in1=ctab[prt, :],
                                                      op=ALU.mult),
                     reads=[PS(b0), ckey], writes=["t1"])
                S.op("dve", lambda e: e.tensor_tensor(out=t2[prt, :], in0=ps[b1][prt, :], in1=stab[prt, :],
                                                      op=ALU.mult),
                     reads=[PS(b1), skey], writes=["t2"])
                S.op("dve", lambda e: e.tensor_tensor(out=out_ap, in0=t1[prt, :], in1=t2[prt, :], op=ALU.add),
                     reads=["t1", "t2"], writes=[out_key])

            def rope_perm(b0, prt, nrow, ctab, stab, ckey, skey, out_ap, out_key, after=None):
                qi = qctr[0] % 2
                qctr[0] += 1
                b1 = nb()
                S.op("act", lambda e: e.activation(out=qbf[qi][prt, :], in_=ps[b0][prt, :], func=AF.Copy),
                     reads=[PS(b0)], writes=[("qbf", qi)])

                def fin():
                    S.op("pe", lambda e: e.matmul(ps[b1][prt, :], lhsT=Pm[prt, 0:nrow], rhs=qbf[qi][prt, :],
                                                  start=True, stop=True),
                         reads=[("qbf", qi), "Pm"], writes=[PS(b1)])
                    rope_evac(b0, b1, prt, ctab, stab, ckey, skey, out_ap, out_key)
                    if after is not None:
                        after()
                pend.append(fin)

            for h in range(8):
                def it(h=h):
                    b0 = nb()
                    mm_group(ps[b0][0:64, :], w_pairs(Win, h * 64, h * 64 + 64), hK + WK(0), b0)
                    rope_perm(b0, slice(0, 64), 64, cA, sA, kcA, ksA, QaT[:, h, :], "QaT")
                items.append(it)
            for g in range(2):
                def it(g=g):
                    b0 = nb()
                    mm_group(ps[b0][0:64, :], w_pairs(Win, 512 + g * 64, 512 + g * 64 + 64), hK + WK(0), b0)
                    rope_perm(b0, slice(0, 64), 64, cA, sA, kcA, ksA, KaT[:, g, :], "KaT")
                items.append(it)

            def it_va():
                b0 = nb()
                for j in range(4):
                    n = 8
                    for kc in range(8):
                        S.op("pe", lambda e, j=j, kc=kc: e.matmul(
                            ps[b0][:, j * 128:(j + 1) * 128], lhsT=h_[:, kc, j * 128:(j + 1) * 128],
                            rhs=Win[:, kc, 640:768], start=(kc == 0), stop=(kc == 7)),
                            reads=hK + WK(0), writes=[PS(b0)])
                S.op("act", lambda e: e.activation(
                    out=Va[:, :, 0:3:2, :],
                    in_=ps[b0][:].rearrange("p (j g d) -> p j g d", j=4, g=2), func=AF.Copy),
                    reads=[PS(b0)], writes=["Va"])
                S.op("sp", lambda e: e.dma_start(out=Vad[:, 4 * c:4 * c + 4, :, :], in_=Va[:]),
                     reads=["Va"], writes=[("Vad", c)], dma=True)
            items.append(it_va)

            def latent(lo, ntile, gvec, gkey, outT, okey, dim):
                def it():
                    banks = []
                    for i in range(ntile):
                        b0 = nb()
                        banks.append(b0)
                        mm_group(ps[b0][:, :], w_pairs(Win, lo + i * 128, lo + (i + 1) * 128), hK + WK(lo), b0)
                        S.op("act", lambda e, i=i, b0=b0: e.activation(out=sq[:, i, :], in_=ps[b0][:, :],
                                                                        func=AF.Square),
                             reads=[PS(b0)], writes=[("sq", i)])
                        S.op("dve", lambda e, i=i, b0=b0: e.tensor_copy(out=qlf[:, i, :], in_=ps[b0][:, :]),
                             reads=[PS(b0)], writes=[("qlf", i)])
                    bs = nb()
                    mm_group(ps[bs][:, :], [(ones_bf[:, :], sq[:, i, :]) for i in range(ntile)],
                             [("sq", i) for i in range(ntile)] + ["ones_bf"], bs)
                    S.op("act", lambda e: e.activation(out=rq[:], in_=ps[bs][:, :], func=AF.Ln,
                                                       scale=1.0 / dim, bias=EPS),
                         reads=[PS(bs)], writes=["rq"])
                    S.op("act", lambda e: e.activation(out=rq[:], in_=rq[:], func=AF.Exp, scale=-0.5),
                         reads=["rq"], writes=["rq"])
                    for i in range(ntile):
                        S.op("dve", lambda e, i=i: e.scalar_tensor_tensor(
                            out=outT[:, i, :], in0=qlf[:, i, :], scalar=gvec[:, i:i + 1], in1=rq[:],
                            op0=ALU.mult, op1=ALU.mult),
                            reads=[("qlf", i), "rq", gkey], writes=[okey])
                return it
            items.append(latent(768, 3, gq, "gq", cqT, "cqT", 384.0))
            items.append(latent(1152, 2, gkv, "gkv", ckvT, "ckvT", 256.0))

            def it_kr():
                b0 = nb()
                mm_group(ps[b0][64:96, :], w_pairs(Win, 1408, 1440), hK + WK(1408), b0)

                def bcast():
                    S.op("act", lambda e: e.activation(
                        out=KT[64:96, :, :], in_=t1[64:96, :].unsqueeze(1).to_broadcast([32, 8, CH]),
                        func=AF.Copy),
                        reads=["t1"], writes=["KT"])
                rope_perm(b0, slice(64, 96), 32, cB, sB, kcB, ksB, t1[64:96, :], "t1", after=bcast)
            items.append(it_kr)

            for h in range(8):
                def it(h=h):
                    b0 = nb()
                    cq_pairs = lambda W, lo, hi: [(W[:, i, lo:hi], cqT[:, i, :]) for i in range(3)]
                    mm_group(ps[b0][0:64, :], cq_pairs(Wuq, h * 96, h * 96 + 64), ["cqT", "Wuq"], b0)
                    mm_group(ps[b0][64:96, :], cq_pairs(Wuq, h * 96 + 64, h * 96 + 96), ["cqT", "Wuq"], b0)
                    S.op("act", lambda e: e.activation(out=QT[0:64, h, :], in_=ps[b0][0:64, :], func=AF.Copy),
                         reads=[PS(b0)], writes=["QT"])
                    rope_perm(b0, slice(64, 96), 32, cB, sB, kcB, ksB, QT[64:96, h, :], "QT")
                items.append(it)

            for h in range(8):
                def it(h=h):
                    b0 = nb()
                    mm_group(ps[b0][0:64, :], [(Wukv[:, i, h * 128:h * 128 + 64], ckvT[:, i, :]) for i in range(2)],
                             ["ckvT", "Wukv"], b0)
                    S.op("act", lambda e: e.activation(out=KT[0:64, h, :], in_=ps[b0][0:64, :], func=AF.Copy),
                         reads=[PS(b0)], writes=["KT"])
                items.append(it)
            for j in range(4):
                def it(j=j):
                    b0 = nb()
                    wv = Wukv[:].rearrange("p k (h f) -> p k h f", f=128)
                    mm_group(ps[b0][:, :], [(ckvT[:, i, j * 128:(j + 1) * 128], wv[:, i, :, 64:128]) for i in range(2)],
                             ["ckvT", "Wukv"], b0)
                    pv4 = ps[b0][:].rearrange("p (pr par d) -> p pr par d", pr=4, par=2)
                    for par in range(2):
                        S.op("dve", lambda e, par=par: e.tensor_copy(
                            out=Vb[:, j, :, 2 * par, :], in_=pv4[:, :, par, :]),
                            reads=[PS(b0)], writes=["Vb"])
                items.append(it)

            for m in range(16):
                def it(m=m):
                    b0 = nb()
                    mm_group(ps[b0][:, :], w_pairs(Win, 1440 + m * 128, 1440 + (m + 1) * 128), hK + WK(1440 + m * 128), b0)
                    G = Ga if m < 8 else Gb
                    S.op("act", lambda e: e.activation(out=G[:, m % 8, :], in_=ps[b0][:, :], func=AF.Sigmoid),
                         reads=[PS(b0)], writes=["Ga" if m < 8 else "Gb"])
                items.append(it)

            def stats(T, tkey, nrow, nh, stat, col0):
                def prep():
                    S.op("act", lambda e: e.activation(out=sq8[0:nrow, 0:nh, :], in_=T[0:nrow, 0:nh, :],
                                                       func=AF.Square),
                         reads=[tkey], writes=["sq8"])

                def pe_():
                    for h in range(nh):
                        b0 = nb()
                        mm_group(ps[b0][:, :], [(ones_bf[0:nrow, :], sq8[0:nrow, h, :])], ["sq8", "ones_bf"], b0)
                        S.op("dve", lambda e, h=h, b0=b0: e.tensor_reduce(
                            out=stat[:, col0 + h:col0 + h + 1], in_=ps[b0][:, :], axis=AX.X, op=ALU.max),
                            reads=[PS(b0)], writes=[("stat", id(stat), col0 + h)])
                return prep, pe_
            stA = stats(QaT, "QaT", 64, 8, statAq, c * 8)
            stB = stats(KaT, "KaT", 64, 2, statAk, c * 2)
            stC = stats(QT, "QT", 96, 8, statBq, c * 8)
            stD = stats(KT, "KT", 96, 8, statBk, c * 8)

            def spill():
                for (dst, src, k, dk) in ((QaTd[:, :, tsl], QaT, "QaT", "QaTd"),
                                          (KaTd[:, :, tsl], KaT, "KaT", "KaTd"),
                                          (QTd[:, :, tsl], QT, "QT", "QTd"),
                                          (KTd[:, :, tsl], KT, "KT", "KTd"),
                                          (Gad[:, :, tsl], Ga, "Ga", "Gad"),
                                          (Gbd[:, :, tsl], Gb, "Gb", "Gbd")):
                    S.op("sp", lambda e, dst=dst, src=src: e.dma_start(out=dst, in_=src[:]),
                         reads=[k], writes=[(dk, c)], dma=True)
                for j in range(4):
                    S.op("dve", lambda e, j=j: e.memset(Vb[:, j, :, 1, :], 1.0), writes=["Vb"])
                S.op("sp", lambda e: e.dma_start(out=Vd[:, 4 * c:4 * c + 4, :, :, :], in_=Vb[:]),
                     reads=["Vb"], writes=[("Vd", c)], dma=True)
            (tab, swaq, swak, va_, latq, latkv, kr_, mlaq, knope, vmla, gates) = (
                items[0:1], items[1:9], items[9:11], items[11:12], items[12:13], items[13:14], items[14:15],
                items[15:23], items[23:31], items[31:35], items[35:51])
            assert len(items) == 51, len(items)
            out_items = (tab + latq + latkv + swaq[0:4] + kr_ + swaq[4:8] + swak + va_ + mlaq + knope + vmla
                         + gates[0:3] + [stA[0]] + gates[3:6] + [stA[1], stB[0]] + gates[6:8] + [stB[1], stC[0]]
                         + gates[8:12] + [stC[1], stD[0]] + gates[12:16] + [stD[1], spill])

            def wrap(it):
                def w():
                    old = list(pend)
                    del pend[:]
                    it()
                    for f in old:
                        f()
                return w
            return [wrap(it) for it in out_items]

        load_tables_for(0)
        for it in norm_items(0):
            it()
        for c in range(NCH):
            nxt = norm_items(c + 1) if c + 1 < NCH else []
            for it in _interleave(proj_items(c), nxt):
                it()
        S.barrier()
        st.close()

    if 2 in phases:
        st = contextlib.ExitStack()
        KTc = sbt(st, "KTc", [128, 8, SEQ], BF16)
        Vc = sbt(st, "Vc", [128, 32, 4, 192], BF16)
        Wosa = sbt(st, "Wosa", [128, 4, DM], BF16)
        Wosb = sbt(st, "Wosb", [128, 4, DM], BF16)
        Wout = sbt(st, "Wout", [128, 8, DM], BF16)
        QaT_2 = sbt(st, "QaT2", [64, 8, CH], BF16)
        QT_2 = sbt(st, "QT2", [128, 8, CH], BF16)
        Gt = [sbt(st, "Gt%d" % i, [128, 2, CH], BF16) for i in range(3)]
        Ka5 = sbt(st, "Ka5", [64, 2, 5 * 128], BF16)
        Va5 = sbt(st, "Va5", [128, 5, 192], BF16)
        OaT = sbt(st, "OaT2", [128, 4, CH], BF16)
        ObT = sbt(st, "ObT2", [128, 4, CH], BF16)
        yT = sbt(st, "yT", [128, 8, CH], BF16)
        pt = [sbt(st, "pt%d" % i, [128, 2, CH], BF16) for i in range(3)]
        rl = sbt(st, "rl", [128, CH], F32)
        xh = [sbt(st, "xh%d" % i, [128, CH], F32) for i in range(3)]
        ty1 = xh[0]
        mdiag = sbt(st, "mdiag", [128, 4, 128], BF16)
        mprev = sbt(st, "mprev", [128, 4, 128], BF16)
        e128 = sbt(st, "e128", [1, 2, 128], BF16)
        sinkP = sbt(st, "sinkP", [1, 8 * 128], BF16)
        sinkB = sbt(st, "sinkB", [128, 8], F32)
        negc = sbt(st, "negc", [128, 4], F32)
        ty2 = xh[1]

        for (dst, src, k) in ((mdiag, mdiag_d, "mdiag"), (mprev, mprev_d, "mprev"), (e128, e128_d, "e128"),
                              (sinkB, sinkb, "sinkB")):
            S.op("sp", lambda e, dst=dst, src=src: e.dma_start(out=dst[:], in_=src), writes=[k], dma=True)
        S.op("sp", lambda e: e.dma_start(out=xh[0][0:1, :], in_=sinkrep[:, 0:512]), writes=[("xh", 0)], dma=True)
        S.op("sp", lambda e: e.dma_start(out=xh[1][0:1, :], in_=sinkrep[:, 512:1024]), writes=[("xh", 1)], dma=True)
        for g in range(2):
            S.op("pool", lambda e, g=g: e.dma_start(
                out=Wosa[g * 64:(g + 1) * 64, :, :],
                in_=w_osa[g * 256:(g + 1) * 256, :].rearrange("(hh d) n -> d hh n", d=64)),
                writes=["Wosa"], dma=True)
        S.op("pool", lambda e: e.dma_start(out=Wosb[:], in_=w_osb.rearrange("(pr p) n -> p pr n", p=128)),
             writes=["Wosb"], dma=True)
        S.op("pool", lambda e: e.dma_start(out=Wout[:], in_=w_out.rearrange("(kc p) n -> p kc n", p=128)),
             writes=["Wout"], dma=True)

        def nop_(fn, r=(), w=("negc",)):
            S.op("dve", fn, reads=list(r) + ["negc"], writes=list(w))
        nop_(lambda e: e.tensor_reduce(out=negc[:, 2:3], in_=statAq[:], axis=AX.X, op=ALU.max))
        nop_(lambda e: e.tensor_reduce(out=negc[:, 3:4], in_=statAk[:], axis=AX.X, op=ALU.max))
        nop_(lambda e: e.tensor_tensor(out=negc[:, 0:1], in0=negc[:, 2:3], in1=negc[:, 3:4], op=ALU.max))
        nop_(lambda e: e.tensor_reduce(out=negc[:, 2:3], in_=sinkB[:], axis=AX.X, op=ALU.max), r=["sinkB"])
        nop_(lambda e: e.scalar_tensor_tensor(out=negc[:, 0:1], in0=negc[:, 0:1], scalar=SCALE_A,
                                              in1=negc[:, 2:3], op0=ALU.mult, op1=ALU.max))
        nop_(lambda e: e.tensor_scalar(out=negc[:, 0:1], in0=negc[:, 0:1], scalar1=-1.0, scalar2=None,
                                       op0=ALU.mult))
        nop_(lambda e: e.tensor_reduce(out=negc[:, 2:3], in_=statBq[:], axis=AX.X, op=ALU.max))
        nop_(lambda e: e.tensor_reduce(out=negc[:, 3:4], in_=statBk[:], axis=AX.X, op=ALU.max))
        nop_(lambda e: e.tensor_tensor(out=negc[:, 1:2], in0=negc[:, 2:3], in1=negc[:, 3:4], op=ALU.max))
        nop_(lambda e: e.tensor_scalar(out=negc[:, 1:2], in0=negc[:, 1:2], scalar1=-SCALE_B, scalar2=None,
                                       op0=ALU.mult))
        for hf in range(2):
            S.op("act", lambda e, hf=hf: e.activation(out=sinkP[:, hf * 512:(hf + 1) * 512], in_=xh[hf][0:1, :],
                                                      func=AF.Exp, bias=negc[0:1, 0:1]),
                 reads=["negc", ("xh", hf)], writes=["sinkP"])

        brow = sbt(st, "brow", [1, CH], BF16)
        negm = sbt(st, "negm", [128, 1], F32)
        for hh_ in range(2):
            S.op("sp", lambda e, hh_=hh_: e.dma_start(out=KTc[96:97, 4 * hh_:4 * hh_ + 4, :],
                                                      in_=onesrow_d[:, 4 * hh_:4 * hh_ + 4, :]),
                 writes=[("KTc", cc) for cc in range(NCH)], dma=True)
        S.op("dve", lambda e: e.tensor_scalar(out=negm[:], in0=negc[:, 1:2], scalar1=1.0 / SCALE_B, scalar2=None,
                                              op0=ALU.mult),
             reads=["negc"], writes=["negm"])
        S.op("dve", lambda e: e.tensor_scalar(out=brow[:], in0=ones_bf[0:1, 0:1].to_broadcast([1, CH]),
                                              scalar1=negm[0:1, 0:1], scalar2=None, op0=ALU.mult),
             reads=["negm", "ones_bf"], writes=["brow"])
        S.op("sp", lambda e: e.dma_start(out=QT_2[96:97, :, :],
                                         in_=brow[0:1, :].unsqueeze(1).to_broadcast([1, 8, CH])),
             reads=["brow"], writes=["QT_2"], dma=True)

        OB = (6, 7)
        octr = [0]

        def load_chunk(c, q):
            tsl = slice(c * CH, (c + 1) * CH)
            S.op(q, lambda e: e.dma_start(out=QaT_2[:], in_=QaTd[:, :, tsl]),
                 reads=[("QaTd", c)], writes=["QaT_2"], dma=True)
            if c == 0:
                S.op(q, lambda e: e.dma_start(out=Ka5[:, :, 128:640], in_=KaTd[:, :, 0:512]),
                     reads=[("KaTd", 0)], writes=["Ka5"], dma=True)
                S.op(q, lambda e: e.dma_start(out=Va5[:, 1:5, :], in_=Vad[:, 0:4, :, :].rearrange("p b s d -> p b (s d)")),
                     reads=[("Vad", 0)], writes=["Va5"], dma=True)
            else:
                S.op(q, lambda e: e.dma_start(out=Ka5[:], in_=KaTd[:, :, c * CH - 128:(c + 1) * CH]),
                     reads=[("KaTd", c), ("KaTd", c - 1)], writes=["Ka5"], dma=True)
                S.op(q, lambda e: e.dma_start(out=Va5[:], in_=Vad[:, 4 * c - 1:4 * c + 4, :, :].rearrange("p b s d -> p b (s d)")),
                     reads=[("Vad", c), ("Vad", c - 1)], writes=["Va5"], dma=True)
            S.op(q, lambda e: e.dma_start(out=KTc[0:96, :, tsl], in_=KTd[:, :, tsl]),
                 reads=[("KTd", c)], writes=[("KTc", c)], dma=True)
            S.op(q, lambda e: e.dma_start(out=Vc[:, 4 * c:4 * c + 4, :, :],
                                          in_=Vd[:, 4 * c:4 * c + 4, :, :, :].rearrange("p b r s d -> p b r (s d)")),
                 reads=[("Vd", c)], writes=[("Vc", c)], dma=True)
            S.op(q, lambda e: e.dma_start(out=QT_2[0:96, :, :], in_=QTd[:, :, tsl]),
                 reads=[("QTd", c)], writes=["QT_2"], dma=True)

        def load_gt(c, m):
            tsl = slice(c * CH, (c + 1) * CH)
            S.op("sp", lambda e: e.dma_start(out=Gt[m % 3][:, 0, :], in_=Gad[:, m, tsl]),
                 reads=[("Gad", c)], writes=[("Gt", m % 3)], dma=True)
            S.op("sp", lambda e: e.dma_start(out=Gt[m % 3][:, 1, :], in_=Gbd[:, m, tsl]),
                 reads=[("Gbd", c)], writes=[("Gt", m % 3)], dma=True)

        def normalize(ob, o_lo, outT, okey, split4, on_act=False):
            l_lo = 64 - o_lo
            lk = ("rl", l_lo)
            if on_act:
                S.op("act", lambda e: e.activation(out=rl[l_lo:l_lo + 64, :], in_=ps[ob][l_lo:l_lo + 64, :],
                                                   func=AF.Ln),
                     reads=[PS(ob)], writes=[lk])
                S.op("act", lambda e: e.activation(out=rl[l_lo:l_lo + 64, :], in_=rl[l_lo:l_lo + 64, :],
                                                   func=AF.Exp, scale=-1.0),
                     reads=[lk], writes=[lk])
            else:
                S.op("dve", lambda e: e.reciprocal(out=rl[l_lo:l_lo + 64, :], in_=ps[ob][l_lo:l_lo + 64, :]),
                     reads=[PS(ob)], writes=[lk])
            a0 = ps[ob][o_lo:o_lo + 64, :]
            a1 = rl[l_lo:l_lo + 64, :]
            if split4:
                a0 = a0.rearrange("p (h q) -> p h q", h=4)
                a1 = a1.rearrange("p (h q) -> p h q", h=4)
            S.op("dve", lambda e: e.tensor_tensor(out=outT, in0=a0, in1=a1, op=ALU.mult),
                 reads=[PS(ob), lk], writes=[okey])

        def PP(P):
            return [PS(2 * P), PS(2 * P + 1)]

        def swa_jobs(c):
            jobs = []
            for g in range(2):
                for j in range(4):
                    subs = [(j, mprev, "mprev")] if (c > 0 or j > 0) else []
                    subs.append((j + 1, mdiag, "mdiag"))

                    def mk(g=g, j=j, subs=subs):
                        ob = OB[octr[0] % 2]
                        octr[0] += 1
                        qv = QaT_2[0:64, 4 * g:4 * g + 4, j * 128:(j + 1) * 128]
                        ns = len(subs)

                        def qk(P):
                            for si, (kb5, mask, mk_) in enumerate(subs):
                                S.op("pe", lambda e, si=si, kb5=kb5: e.matmul(
                                    pp[P][:, si, :], lhsT=Ka5[:, g, kb5 * 128:(kb5 + 1) * 128], rhs=qv,
                                    start=True, stop=False),
                                    reads=["Ka5", "QaT_2"], writes=[PS(2 * P + si)])
                                S.op("pe", lambda e, si=si, mask=mask: e.matmul(
                                    pp[P][:, si, :], lhsT=ident[:], rhs=mask[:], start=False, stop=True),
                                    reads=["ident", mk_], writes=[PS(2 * P + si)])

                        def ex(P):
                            S.op("act", lambda e: e.activation(out=pt[P][:, 0:ns, :], in_=pp[P][:, 0:ns, :],
                                                               func=AF.Exp, scale=SCALE_A, bias=negc[:, 0:1]),
                                 reads=PP(P)[0:ns] + ["negc"], writes=[("pt", P)])

                        def pv(P):
                            for si, (kb5, mask, mk_) in enumerate(subs):
                                S.op("pe", lambda e, si=si, kb5=kb5: e.matmul(
                                    ps[ob][:, :], lhsT=Va5[:, kb5, g * 64:g * 64 + 128], rhs=pt[P][:, si, :],
                                    start=(si == 0), stop=False),
                                    reads=["Va5", ("pt", P)], writes=[PS(ob)])
                            S.op("pe", lambda e: e.matmul(
                                ps[ob][:, :], lhsT=e128[0:1, g, :], rhs=sinkP[0:1, 4 * g * 128:(4 * g + 4) * 128],
                                start=False, stop=True),
                                reads=["e128", "sinkP"], writes=[PS(ob)])
                            normalize(ob, 64 * g, OaT[64 * g:64 * g + 64, :, j * 128:(j + 1) * 128], "OaT", True,
                                      on_act=(j % 2 == 0))
                        return dict(qk=qk, ex=ex, pv=pv)
                    jobs.append(mk())
            return jobs

        def mla_jobs(c):
            jobs = []
            nk = 4 * c + 4
            KTk = [("KTc", cc) for cc in range(c + 1)]
            Vk = [("Vc", cc) for cc in range(c + 1)]
            for h in range(8):
                ob = OB[octr[0] % 2]
                octr[0] += 1
                pr, par = h // 2, h % 2
                for t in range(nk // 2):
                    kbs = (2 * t, 2 * t + 1)
                    diag = kbs[0] >= 4 * c

                    def mk(h=h, ob=ob, pr=pr, par=par, kbs=kbs, diag=diag):
                        def geom(kb):
                            j = kb - 4 * c
                            q0 = max(j, 0) * 128
                            return q0, CH - q0

                        def qk(P):
                            for si, kb in enumerate(kbs):
                                q0, n = geom(kb)
                                S.op("pe", lambda e, si=si, kb=kb, q0=q0, n=n: e.matmul(
                                    pp[P][:, si, 0:n], lhsT=KTc[0:97, h, kb * 128:(kb + 1) * 128],
                                    rhs=QT_2[0:97, h, q0:CH], start=True, stop=(not diag)),
                                    reads=KTk + ["QT_2"], writes=[PS(2 * P + si)])
                                if diag:
                                    S.op("pe", lambda e, si=si: e.matmul(
                                        pp[P][:, si, 0:128], lhsT=ident[:], rhs=mdiag[:, 0, :],
                                        start=False, stop=True),
                                        reads=["ident", "mdiag"], writes=[PS(2 * P + si)])

                        def ex(P):
                            if not diag:
                                S.op("act", lambda e: e.activation(out=pt[P][:], in_=pp[P][:], func=AF.Exp,
                                                                   scale=SCALE_B),
                                     reads=PP(P), writes=[("pt", P)])
                            else:
                                for si, kb in enumerate(kbs):
                                    q0, n = geom(kb)
                                    S.op("act", lambda e, si=si, n=n: e.activation(
                                        out=pt[P][:, si, 0:n], in_=pp[P][:, si, 0:n], func=AF.Exp,
                                        scale=SCALE_B),
                                        reads=[PS(2 * P + si)], writes=[("pt", P)])

                        def pv(P):
                            for si, kb in enumerate(kbs):
                                q0, n = geom(kb)
                                last = (kb == nk - 1)
                                S.op("pe", lambda e, si=si, kb=kb, q0=q0, n=n, last=last: e.matmul(
                                    ps[ob][:, q0:CH], lhsT=Vc[:, kb, pr, par * 64:par * 64 + 128],
                                    rhs=pt[P][:, si, 0:n], start=(kb == 0), stop=last),
                                    reads=Vk + [("pt", P)], writes=[PS(ob)])
                            if kbs[1] == nk - 1:
                                normalize(ob, 64 * par, ObT[64 * par:64 * par + 64, pr, :], "ObT", False,
                                          on_act=(c <= 1))
                        return dict(qk=qk, ex=ex, pv=pv)
                    jobs.append(mk())
            return jobs

        def outproj_jobs(c):
            jobs = []
            for j in range(4):
                for n_ in range(2):
                    def mk(j=j, n_=n_):
                        s_ = 2 * j + n_
                        hb = s_ % 3
                        t0 = c * CH + j * 128
                        nsl = slice(n_ * 512, (n_ + 1) * 512)

                        def qk(P):
                            S.op("sp", lambda e: e.dma_start(out=xh[hb][:], in_=x[t0:t0 + 128, nsl]),
                                 writes=[("xh", hb)], dma=True)
                            mm_group(pp[P][:, 0, :], [(yT[:, kc, j * 128:(j + 1) * 128], Wout[:, kc, nsl])
                                                      for kc in range(8)], ["yT", "Wout"], 2 * P)

                        def ex(P):
                            S.op("dve", lambda e: e.tensor_tensor(out=xh[hb][:], in0=pp[P][:, 0, :], in1=xh[hb][:],
                                                                  op=ALU.add),
                                 reads=[PS(2 * P), ("xh", hb)], writes=[("xh", hb)])
                            S.op("sp", lambda e: e.dma_start(out=x1d[t0:t0 + 128, nsl], in_=xh[hb][:]),
                                 reads=[("xh", hb)], writes=[("x1d", c, j, n_)], dma=True)

                        def pv(P):
                            pass
                        return dict(qk=qk, ex=ex, pv=pv)
                    jobs.append(mk())
            return jobs

        LOOK = 2

        def run_jobs(jobs, pctr):
            n = len(jobs)
            base = pctr[0]
            for k in range(n + LOOK):
                if k < n:
                    jobs[k]["qk"]((base + k) % 3)
                if k - LOOK >= 0:
                    P = (base + k - LOOK) % 3
                    jobs[k - LOOK]["ex"](P)
                    jobs[k - LOOK]["pv"](P)
            pctr[0] = base + n

        pctr = [0]
        load_chunk(0, "sp")
        for m in range(3):
            load_gt(0, m)

        def chunk2(c):
            mj = mla_jobs(c)
            if c > 0:
                mj = _interleave(mj, outproj_jobs(c - 1))
            run_jobs(swa_jobs(c) + mj, pctr)
            if c + 1 < NCH:
                load_chunk(c + 1, "pool")

            for m in range(8):
                P = m % 3
                msl = slice(m * 128, (m + 1) * 128)
                mm_group(pp[P][:, 0, :], [(Wosa[:, hh, msl], OaT[:, hh, :]) for hh in range(4)],
                         ["Wosa", "OaT"], 2 * P)
                mm_group(pp[P][:, 1, :], [(Wosb[:, pr, msl], ObT[:, pr, :]) for pr in range(4)],
                         ["Wosb", "ObT"], 2 * P + 1)
                S.op("dve", lambda e, P=P, m=m: e.tensor_tensor(out=ty1[:], in0=pp[P][:, 0, :], in1=Gt[m % 3][:, 0, :],
                                                                op=ALU.mult),
                     reads=[PS(2 * P), ("Gt", m % 3)], writes=[("xh", 0)])
                S.op("dve", lambda e, P=P, m=m: e.tensor_tensor(out=ty2[:], in0=pp[P][:, 1, :], in1=Gt[m % 3][:, 1, :],
                                                                op=ALU.mult),
                     reads=[PS(2 * P + 1), ("Gt", m % 3)], writes=[("xh", 1)])
                S.op("dve", lambda e, m=m: e.tensor_tensor(out=yT[:, m, :], in0=ty1[:], in1=ty2[:], op=ALU.add),
                     reads=[("xh", 0), ("xh", 1)], writes=["yT"])
                if m + 3 < 8:
                    load_gt(c, m + 3)
            if c + 1 < NCH:
                for m in range(3):
                    load_gt(c + 1, m)

            if debug and c == 0:
                S.op("sp", lambda e: e.dma_start(out=dOaT, in_=OaT[:]), reads=["OaT"], dma=True)
                S.op("sp", lambda e: e.dma_start(out=dObT, in_=ObT[:]), reads=["ObT"], dma=True)
                S.op("sp", lambda e: e.dma_start(out=dyT, in_=yT[:]), reads=["yT"], dma=True)
                S.op("sp", lambda e: e.dma_start(out=dVc, in_=Vc[:, 0:4, :, :]), reads=[("Vc", 0)], dma=True)
                S.op("sp", lambda e: e.dma_start(out=dKT, in_=KTc[:, :, 0:CH]), reads=[("KTc", 0)], dma=True)
                S.op("sp", lambda e: e.dma_start(out=dQT, in_=QT_2[:]), reads=["QT_2"], dma=True)
            if c == NCH - 1:
                run_jobs(outproj_jobs(c), pctr)

        for c in range(NCH):
            chunk2(c)
        S.barrier()
        st.close()

    if 3 in phases:
        st = contextlib.ExitStack()
        Wg = sbt(st, "Wg", [128, 8, DFF], BF16)
        Wu = sbt(st, "Wu", [128, 8, DFF], BF16)
        Wd = sbt(st, "Wd", [128, NFF, DM], BF16)
        gffn = sbt(st, "gffn", [128, 8], F32)
        gfin = sbt(st, "gfin", [128, DM], F32)
        xb = [sbt(st, "fxb%d" % i, [128, DM], F32) for i in range(2)]
        junk = sbt(st, "fjunk", [128, DM], BF16)
        xs = [sbt(st, "fxs%d" % i, [128, DM], BF16) for i in range(2)]
        hT = [sbt(st, "fhT%d" % i, [128, 8, CH], BF16) for i in range(2)]
        actT = sbt(st, "actT", [128, NFF, CH], BF16)
        sg = [sbt(st, "sg%d" % i, [128, CH], F32) for i in range(2)]
        xr = sbt(st, "xr", [128, DM], F32)
        x2 = sbt(st, "x2", [128, DM], F32)
        ob_ = sbt(st, "ob_", [128, DM], F32)
        junk2 = sbt(st, "junk2", [128, DM], BF16)
        ss2 = sbt(st, "ss2", [128, 1], F32)
        rs2 = sbt(st, "rs2", [128, 1], F32)

        FG = ((0, 768), (768, 1536), (1536, 2176), (2176, DFF))

        def FGRP(m):
            return [gi for gi, (a, b_) in enumerate(FG) if a <= m * 128 < b_][0]
        for gi, (a, b_) in enumerate(FG):
            S.op("pool", lambda e, a=a, b_=b_: e.dma_start(
                out=Wg[:, :, a:b_], in_=w_gate[:, a:b_].rearrange("(kc p) n -> p kc n", p=128)),
                writes=[("Wgu", gi)], dma=True)
            S.op("pool", lambda e, a=a, b_=b_: e.dma_start(
                out=Wu[:, :, a:b_], in_=w_up[:, a:b_].rearrange("(kc p) n -> p kc n", p=128)),
                writes=[("Wgu", gi)], dma=True)
        for (m0, m1) in ((0, 6), (6, 12), (12, 17), (17, NFF)):
            S.op("pool", lambda e, m0=m0, m1=m1: e.dma_start(
                out=Wd[:, m0:m1, :], in_=w_down[m0 * 128:m1 * 128, :].rearrange("(m p) n -> p m n", p=128)),
                writes=["Wd"], dma=True)
        S.op("sp", lambda e: e.dma_start(out=gffn[:], in_=ffn_g.rearrange("(kc p) -> p kc", p=128),
                                         allow_slow_non_contiguous=True), writes=["gT"], dma=True)
        S.op("sp", lambda e: e.dma_start(out=gfin[:], in_=fin_g.partition_broadcast(128)),
             writes=["gfin"], dma=True)

        def ffn_norm_items(c):
            return norm_sched([norm_block(x1d, c * CH + j * 128, xb, junk, xs,
                                          hT[c % 2][:, :, j * 128:(j + 1) * 128], ("fhT", c % 2, j), gffn)
                               for j in range(4)])

        def ffn_items(c):
            items = []
            hK = [("fhT", c % 2, j) for j in range(4)]
            h_ = hT[c % 2]
            for m in range(NFF):
                def it(m=m):
                    bg, bu = nb(), nb()
                    msl = slice(m * 128, (m + 1) * 128)
                    wk = [("Wgu", FGRP(m))]
                    mm_group(ps[bg][:, :], [(Wg[:, kc, msl], h_[:, kc, :]) for kc in range(8)], hK + wk, bg)
                    mm_group(ps[bu][:, :], [(Wu[:, kc, msl], h_[:, kc, :]) for kc in range(8)], hK + wk, bu)
                    s_ = m % 2
                    S.op("act", lambda e: e.activation(out=sg[s_][:], in_=ps[bg][:, :], func=AF.Silu),
                         reads=[PS(bg)], writes=[("sg", s_)])
                    S.op("dve", lambda e: e.tensor_tensor(out=actT[:, m, :], in0=ps[bu][:, :],
                                                          in1=sg[s_][:], op=ALU.mult),
                         reads=[PS(bu), ("sg", s_)], writes=[("actT", m)])
                items.append(it)
            aK = [("actT", m) for m in range(NFF)]
            for j in range(4):
                def it(j=j):
                    t0 = c * CH + j * 128
                    S.op("sp", lambda e: e.dma_start(out=xr[:], in_=x1d[t0:t0 + 128, :]),
                         reads=[("x1d", c, j, 0), ("x1d", c, j, 1)], writes=["xr"], dma=True)
                    for n_ in range(2):
                        bo = nb()
                        mm_group(ps[bo][:, :], [(actT[:, m, j * 128:(j + 1) * 128], Wd[:, m, n_ * 512:(n_ + 1) * 512])
                                                for m in range(NFF)], aK + ["Wd"], bo)
                        S.op("dve", lambda e, bo=bo, n_=n_: e.tensor_tensor(
                            out=x2[:, n_ * 512:(n_ + 1) * 512], in0=ps[bo][:, :], in1=xr[:, n_ * 512:(n_ + 1) * 512],
                            op=ALU.add),
                            reads=[PS(bo), "xr"], writes=["x2"])
                    S.op("act", lambda e: e.activation(out=junk2[:], in_=x2[:], func=AF.Square, accum_out=ss2[:]),
                         reads=["x2"], writes=["junk2", "ss2"])
                    S.op("act", lambda e: e.activation(out=rs2[:], in_=ss2[:], func=AF.Ln, scale=1.0 / DM, bias=EPS),
                         reads=["ss2"], writes=["rs2"])
                    S.op("act", lambda e: e.activation(out=rs2[:], in_=rs2[:], func=AF.Exp, scale=-0.5),
                         reads=["rs2"], writes=["rs2"])
                    S.op("dve", lambda e: e.scalar_tensor_tensor(out=ob_[:], in0=x2[:], scalar=rs2[:, 0:1], in1=gfin[:],
                                                                 op0=ALU.mult, op1=ALU.mult),
                         reads=["x2", "rs2", "gfin"], writes=["ob_"])
                    S.op("sp", lambda e: e.dma_start(out=out[t0:t0 + 128, :], in_=ob_[:]),
                         reads=["ob_"], dma=True)
                items.append(it)
            return items

        for it in ffn_norm_items(0):
            it()
        for c in range(NCH):
            nxt = ffn_norm_items(c + 1) if c + 1 < NCH else []
            for it in _interleave(ffn_items(c), nxt):
                it()
        st.close()

    S.emit()
    top.close()
    return nc


def _consts():
    bf = ml_dtypes.bfloat16
    pos = np.arange(SEQ, dtype=np.float32)[:, None]

    def tables(dim):
        inv = (10000.0 ** (-np.arange(0, dim, 2, dtype=np.float32) / dim)).astype(np.float32)
        ang = (pos * inv[None, :]).astype(np.float32)
        c = np.cos(ang).astype(np.float32).T
        s = np.sin(ang).astype(np.float32).T
        return (np.ascontiguousarray(np.concatenate([c, c], 0)),
                np.ascontiguousarray(np.concatenate([-s, s], 0)))

    cosA, sinA = tables(64)
    cosB, sinB = tables(32)
    k = np.arange(128)[:, None]
    q = np.arange(128)[None, :]
    mdiag = np.where(k <= q, 0.0, NEG).astype(np.float32)
    mprev = np.where(k > q, 0.0, NEG).astype(np.float32)
    e128 = np.zeros((1, 2, 128), np.float32)
    e128[0, 0, 64:] = 1.0
    e128[0, 1, :64] = 1.0
    return {
        "cosA": cosA, "sinA": sinA, "cosB": cosB, "sinB": sinB,
        "ident": np.eye(128, dtype=np.float32).astype(bf),
        "mdiag4": np.ascontiguousarray(np.broadcast_to(mdiag[:, None, :], (128, 4, 128))).astype(bf),
        "mprev4": np.ascontiguousarray(np.broadcast_to(mprev[:, None, :], (128, 4, 128))).astype(bf),
        "e128": e128.astype(bf),
        "ropeperm": _ropeperm().astype(bf),
        "onesrow": np.ones((1, 8, SEQ), np.float32).astype(bf),
    }


def _ropeperm():
    p = np.zeros((128, 64), np.float32)
    for m in range(64):
        p[(m + 32) % 64, m] = 1.0
    for m in range(32):
        p[64 + (m + 16) % 32, m] = 1.0
    return p


_NC_CACHE = {}


def kernel(x, mix_norm_g, w_in, swa_sinks, q_norm_g, w_uq, kv_norm_g, w_ukv,
           w_o_swa, w_o_mla, w_out, ffn_norm_g, w_gate, w_up, w_down, final_norm_g,
           _debug=False, _phases=(1, 2, 3)):
    f = lambda a: np.ascontiguousarray(np.asarray(a, dtype=np.float32))
    x = f(x)
    sinks = f(swa_sinks)[0]
    shared = {
        "mix_norm_g": f(mix_norm_g)[0], "w_in": f(w_in)[0], "q_norm_g": f(q_norm_g)[0],
        "w_uq": f(w_uq)[0], "kv_norm_g": f(kv_norm_g)[0], "w_ukv": f(w_ukv)[0],
        "w_o_swa": f(w_o_swa)[0], "w_o_mla": f(w_o_mla)[0], "w_out": f(w_out)[0],
        "ffn_norm_g": f(ffn_norm_g)[0], "w_gate": f(w_gate)[0], "w_up": f(w_up)[0],
        "w_down": f(w_down)[0], "final_norm_g": f(final_norm_g),
        "sinkrep": np.ascontiguousarray(np.repeat(sinks, 128)[None, :]),
        "sinkb": np.ascontiguousarray(np.broadcast_to(sinks[None, :], (128, 8))),
    }
    shared.update(_consts())
    nc = build_nc(debug=_debug, phases=_phases)
    in_maps = []
    for b in range(8):
        m = dict(shared)
        m["x"] = np.ascontiguousarray(x[b])
        in_maps.append(m)
    res = run_bass_kernel_spmd(nc, in_maps, core_ids=list(range(8)))
    if _debug:
        return res.results
    return np.stack([np.asarray(r["out"], dtype=np.float32) for r in res.results], axis=0)
```

```python
import contextlib
import numpy as np
import ml_dtypes
import concourse.bass as bass
import concourse.mybir as mybir
from concourse.bass_utils import run_bass_kernel_spmd

F32, BF16 = mybir.dt.float32, mybir.dt.bfloat16
AF = mybir.ActivationFunctionType
ALU = mybir.AluOpType
AX = mybir.AxisListType

SEQ = 4096
DM = 1024
import os
NCH = int(os.environ.get('K_NCH', '8'))
CH = 512
DFF = 2816
NFF = 22
SCALE_A = 64 ** -0.5
SCALE_B = 96 ** -0.5
NEG = -30000.0
EPS = 1e-6

ENGS = ("pe", "act", "dve", "pool", "sp")
N_DMA_SEMS = 24


class _Op:
    __slots__ = ("eng", "fn", "deps", "is_dma", "signal", "seq", "dsem", "dval", "pre_wait")

    def __init__(self, eng, fn, is_dma):
        self.eng = eng
        self.fn = fn
        self.is_dma = is_dma
        self.deps = []
        self.signal = False
        self.seq = None
        self.dsem = None
        self.dval = None
        self.pre_wait = None


class Sched:
    def __init__(self, nc):
        self.nc = nc
        self.ops = {e: [] for e in ENGS}
        self.all = []
        self.last_w = {}
        self.readers = {}
        self.dma_i = 0
        self.dma_ip = 0
        self.dma_last = [None] * N_DMA_SEMS
        self.dma_val = [0] * N_DMA_SEMS
        self.last_eng = {e: None for e in ENGS}
        self.pending_barrier = {e: [] for e in ENGS}

    def barrier(self):
        deps = [o for o in self.last_eng.values() if o is not None]
        deps += [o for o in self.dma_last if o is not None]
        for d in deps:
            d.signal = True
        for e in ENGS:
            self.pending_barrier[e] = list(deps)

    def op(self, eng, fn, reads=(), writes=(), dma=False):
        o = _Op(eng, fn, dma)
        psr = [k for k in reads if isinstance(k, tuple) and k[0] in ("ps", "pT")]
        if psr:
            reads = [k for k in reads if k not in psr]
            writes = list(writes) + psr
        deps = set()
        for k in reads:
            w = self.last_w.get(k)
            if w is not None:
                deps.add(w)
        for k in writes:
            w = self.last_w.get(k)
            if w is not None:
                deps.add(w)
            for r in self.readers.get(k, ()):
                deps.add(r)
        if self.pending_barrier[eng]:
            deps.update(self.pending_barrier[eng])
            self.pending_barrier[eng] = []
        for d in deps:
            if d.eng == "pe" and eng == "pe" and not d.is_dma and not dma:
                continue
            o.deps.append(d)
            d.signal = True
        for k in writes:
            self.last_w[k] = o
            self.readers[k] = []
        for k in reads:
            if k in writes:
                continue
            self.readers.setdefault(k, []).append(o)
        if dma:
            if eng == "pool":
                s = 16 + self.dma_ip % 8
                self.dma_ip += 1
            else:
                s = self.dma_i % 16
                self.dma_i += 1
            o.pre_wait = self.dma_last[s]
            self.dma_val[s] += 16
            o.dsem = s
            o.dval = self.dma_val[s]
            self.dma_last[s] = o
        else:
            self.last_eng[eng] = o
        self.ops[eng].append(o)
        self.all.append(o)
        return o

    def emit(self):
        nc = self.nc
        cnt = {e: 0 for e in ENGS}
        for o in self.all:
            if (not o.is_dma) and o.signal:
                cnt[o.eng] += 1
                o.seq = cnt[o.eng]
        dma_val = self.dma_val
        with contextlib.ExitStack() as st:
            esem = {e: st.enter_context(nc.semaphore("s_" + e)) for e in ENGS}
            dsem = [st.enter_context(nc.semaphore("d%d" % i)) for i in range(N_DMA_SEMS)]
            block = st.enter_context(nc.Block())

            def run(engname, eng):
                waited = {}

                def wait(key, sem, val):
                    if waited.get(key, 0) >= val:
                        return
                    waited[key] = val
                    eng.wait_ge(sem, val)

                for o in self.ops[engname]:
                    for d in o.deps:
                        if d.is_dma:
                            wait(("d", d.dsem), dsem[d.dsem], d.dval)
                        else:
                            wait(("e", d.eng), esem[d.eng], d.seq)
                    if o.is_dma:
                        p = o.pre_wait
                        if p is not None:
                            wait(("d", p.dsem), dsem[p.dsem], p.dval)
                        ins = o.fn(eng)
                        ins.then_inc(dsem[o.dsem], 16)
                    else:
                        ins = o.fn(eng)
                        if o.signal:
                            ins.then_inc(esem[engname], 1)
                if engname == "sp":
                    for s in range(N_DMA_SEMS):
                        if dma_val[s] > 0:
                            wait(("d", s), dsem[s], dma_val[s])

            @block.tensor
            def _(e):
                run("pe", e)

            @block.scalar
            def _(e):
                run("act", e)

            @block.vector
            def _(e):
                run("dve", e)

            @block.gpsimd
            def _(e):
                run("pool", e)

            @block.sync
            def _(e):
                run("sp", e)


def _interleave(a, b):
    if not b:
        return list(a)
    out = []
    step = max(1, len(a) // (len(b) + 1))
    bi = 0
    for i, it in enumerate(a):
        out.append(it)
        if bi < len(b) and (i + 1) % step == 0:
            out.append(b[bi])
            bi += 1
    out.extend(b[bi:])
    return out


def build_nc(debug=False, phases=(1, 2, 3)):
    nc = bass.Bass("TRN2", target_bir_lowering=False)

    def din(name, shape, dt=F32):
        return nc.dram_tensor(name, list(shape), dt, kind="ExternalInput").ap()

    x = din("x", [SEQ, DM])
    mix_g = din("mix_norm_g", [DM])
    w_in = din("w_in", [DM, 3488])
    q_g = din("q_norm_g", [384])
    w_uq = din("w_uq", [384, 768])
    kv_g = din("kv_norm_g", [256])
    w_ukv = din("w_ukv", [256, 1024])
    w_osa = din("w_o_swa", [512, DM])
    w_osb = din("w_o_mla", [512, DM])
    w_out = din("w_out", [DM, DM])
    ffn_g = din("ffn_norm_g", [DM])
    w_gate = din("w_gate", [DM, DFF])
    w_up = din("w_up", [DM, DFF])
    w_down = din("w_down", [DFF, DM])
    fin_g = din("final_norm_g", [DM])
    sinkrep = din("sinkrep", [1, 8 * 128])
    sinkb = din("sinkb", [128, 8])
    cosA = din("cosA", [64, SEQ])
    sinA = din("sinA", [64, SEQ])
    cosB = din("cosB", [32, SEQ])
    sinB = din("sinB", [32, SEQ])
    ident_d = din("ident", [128, 128], BF16)
    mdiag_d = din("mdiag4", [128, 4, 128], BF16)
    mprev_d = din("mprev4", [128, 4, 128], BF16)
    e128_d = din("e128", [1, 2, 128], BF16)
    perm_d = din("ropeperm", [128, 64], BF16)
    onesrow_d = din("onesrow", [1, 8, SEQ], BF16)

    out = nc.dram_tensor("out", [SEQ, DM], F32, kind="ExternalOutput").ap()

    skind = "ExternalOutput" if debug else "Internal"

    def dscr(name, shape, dt):
        return nc.dram_tensor(name, list(shape), dt, kind=skind).ap()

    QaTd = dscr("QaTd", [64, 8, SEQ], BF16)
    KaTd = dscr("KaTd", [64, 2, SEQ], BF16)
    Vad = dscr("Vad", [128, 32, 3, 64], BF16)
    QTd = dscr("QTd", [96, 8, SEQ], BF16)
    KTd = dscr("KTd", [96, 8, SEQ], BF16)
    Vd = dscr("Vd", [128, 32, 4, 3, 64], BF16)
    Gad = dscr("Gad", [128, 8, SEQ], BF16)
    Gbd = dscr("Gbd", [128, 8, SEQ], BF16)
    x1d = dscr("x1d", [SEQ, DM], F32)
    if debug:
        dOaT = dscr("dOaT", [128, 4, CH], BF16)
        dObT = dscr("dObT", [128, 4, CH], BF16)
        dyT = dscr("dyT", [128, 8, CH], BF16)
        dVc = dscr("dVc", [128, 4, 4, 192], BF16)
        dKT = dscr("dKT", [128, 8, CH], BF16)
        dQT = dscr("dQT", [128, 8, CH], BF16)

    S = Sched(nc)
    top = contextlib.ExitStack()

    def sbt(stack, name, shape, dt):
        return stack.enter_context(nc.sbuf_tensor("sb_" + name, list(shape), dt))

    pp = [top.enter_context(nc.psum_tensor("pp%d" % i, [128, 2, 512], F32)) for i in range(3)]
    ps = [pp[i // 2][:, i % 2, :] for i in range(6)]
    ps += [top.enter_context(nc.psum_tensor("ps%d" % i, [128, 512], F32))[:] for i in (6, 7)]
    pT = [ps[6 + i].bitcast(BF16).rearrange("p (k c) -> p k c", k=8) for i in range(2)]

    ident = sbt(top, "ident", [128, 128], BF16)
    ones_bf = sbt(top, "ones_bf", [128, 128], BF16)
    statAq = sbt(top, "statAq", [128, 64], F32)
    statAk = sbt(top, "statAk", [128, 16], F32)
    statBq = sbt(top, "statBq", [128, 64], F32)
    statBk = sbt(top, "statBk", [128, 64], F32)
    ss = sbt(top, "ss", [128, 2], F32)
    rstd = sbt(top, "rstd", [128, 2], F32)

    S.op("sp", lambda e: e.dma_start(out=ident[:], in_=ident_d), writes=["ident"], dma=True)
    S.op("pool", lambda e: e.memset(ones_bf[:], 1.0), writes=["ones_bf"])
    for stt in (statAq, statAk, statBq, statBk):
        S.op("pool", lambda e, stt=stt: e.memset(stt[:], 0.0), writes=[("statinit", id(stt))])

    bank_ctr = [0]

    def nb():
        b = bank_ctr[0] % 6
        bank_ctr[0] += 1
        return b

    def PS(b):
        return ("ps", b)

    blk_ctr = [0]

    def norm_block(src, t0, xb, junk, xs, hT_ap, hT_key, gT):
        b = blk_ctr[0] % 2
        blk_ctr[0] += 1

        def prep():
            S.op("sp", lambda e: e.dma_start(out=xb[b][:], in_=src[t0:t0 + 128, :]),
                 writes=[("xb", b)], dma=True)
            S.op("act", lambda e: e.activation(out=junk[:], in_=xb[b][:], func=AF.Square,
                                               accum_out=ss[:, b:b + 1]),
                 reads=[("xb", b)], writes=["junk", ("ss", b)])
            S.op("act", lambda e: e.activation(out=rstd[:, b:b + 1], in_=ss[:, b:b + 1], func=AF.Ln,
                                               scale=1.0 / DM, bias=EPS),
                 reads=[("ss", b)], writes=[("rstd", b)])
            S.op("act", lambda e: e.activation(out=rstd[:, b:b + 1], in_=rstd[:, b:b + 1], func=AF.Exp,
                                               scale=-0.5),
                 reads=[("rstd", b)], writes=[("rstd", b)])
            S.op("dve", lambda e: e.tensor_scalar(out=xs[b][:], in0=xb[b][:], scalar1=rstd[:, b:b + 1],
                                                  scalar2=None, op0=ALU.mult),
                 reads=[("xb", b), ("rstd", b)], writes=[("xs", b)])

        def xpose():
            for kc in range(8):
                S.op("pe", lambda e, kc=kc: e.transpose(out=pT[b][:, kc, :],
                                                        in_=xs[b][:, kc * 128:(kc + 1) * 128],
                                                        identity=ident[:]),
                     reads=[("xs", b), "ident"], writes=[PS(6 + b)])
            S.op("dve", lambda e: e.tensor_tensor(out=hT_ap, in0=pT[b],
                                                  in1=gT[:].unsqueeze(2).to_broadcast([128, 8, 128]),
                                                  op=ALU.mult),
                 reads=[PS(6 + b), "gT"], writes=[hT_key])
        return prep, xpose

    def norm_sched(blocks):
        p = [b_[0] for b_ in blocks]
        x_ = [b_[1] for b_ in blocks]
        return [p[0], p[1], x_[0], p[2], x_[1], p[3], x_[2], x_[3]]

    def mm_group(out_ap, pairs, reads, bank, extra=()):
        n = len(pairs) + len(extra)
        i = 0
        for (l, r) in list(pairs) + list(extra):
            S.op("pe", lambda e, l=l, r=r, i=i: e.matmul(out_ap, lhsT=l, rhs=r,
                                                         start=(i == 0), stop=(i == n - 1)),
                 reads=reads, writes=[PS(bank)])
            i += 1

    if 1 in phases:
        st = contextlib.ExitStack()
        Win = sbt(st, "Win", [128, 8, 3488], BF16)
        Wsw = sbt(st, "Wsw", [128, 8, 672], BF16)
        Wuq = sbt(st, "Wuq", [128, 3, 768], BF16)
        Wuqsw = sbt(st, "Wuqsw", [128, 3, 256], BF16)
        Wukv = sbt(st, "Wukv", [128, 2, 1024], BF16)
        gmix = sbt(st, "gmix", [128, 8], F32)
        gq = sbt(st, "gq", [128, 3], F32)
        gkv = sbt(st, "gkv", [128, 2], F32)
        xb = [sbt(st, "xb%d" % i, [128, DM], F32) for i in range(2)]
        junk = sbt(st, "junk", [128, DM], BF16)
        xs = [sbt(st, "xs%d" % i, [128, DM], BF16) for i in range(2)]
        hT = [sbt(st, "hT%d" % i, [128, 8, CH], BF16) for i in range(2)]
        cA2 = [sbt(st, "cA%d" % i, [64, CH], F32) for i in range(2)]
        sA2 = [sbt(st, "sA%d" % i, [64, CH], F32) for i in range(2)]
        cB2 = [sbt(st, "cB%d" % i, [128, CH], F32) for i in range(2)]
        sB2 = [sbt(st, "sB%d" % i, [128, CH], F32) for i in range(2)]

        def load_tables_for(cc):
            tsl_ = slice(cc * CH, (cc + 1) * CH)
            i_ = cc % 2
            S.op("sp", lambda e: e.dma_start(out=cA2[i_][:], in_=cosA[:, tsl_]), writes=[("cA", i_)], dma=True)
            S.op("sp", lambda e: e.dma_start(out=sA2[i_][:], in_=sinA[:, tsl_]), writes=[("sA", i_)], dma=True)
            S.op("sp", lambda e: e.dma_start(out=cB2[i_][64:96, :], in_=cosB[:, tsl_]), writes=[("cB", i_)], dma=True)
            S.op("sp", lambda e: e.dma_start(out=sB2[i_][64:96, :], in_=sinB[:, tsl_]), writes=[("sB", i_)], dma=True)
        QaT = sbt(st, "QaT", [64, 8, CH], BF16)
        KaT = sbt(st, "KaT", [64, 2, CH], BF16)
        Va = sbt(st, "Va", [128, 4, 3, 64], BF16)
        qlf = sbt(st, "qlf", [128, 3, CH], F32)
        sq = sbt(st, "sq", [128, 3, CH], BF16)
        rq = sbt(st, "rq", [128, CH], F32)
        cqT = sbt(st, "cqT", [128, 3, CH], BF16)
        ckvT = sbt(st, "ckvT", [128, 2, CH], BF16)
        QT = sbt(st, "QT", [96, 8, CH], BF16)
        KT = sbt(st, "KT", [96, 8, CH], BF16)
        Vb = sbt(st, "Vb", [128, 4, 4, 3, 64], BF16)
        Ga = sbt(st, "Ga", [128, 8, CH], BF16)
        Gb = sbt(st, "Gb", [128, 8, CH], BF16)
        t1 = sbt(st, "t1", [128, CH], F32)
        t2 = sbt(st, "t2", [128, CH], F32)
        sq8 = sbt(st, "sq8", [128, 8, CH], BF16)
        Pm = sbt(st, "Pm", [128, 64], BF16)
        qbf = [sbt(st, "qbf%d" % i, [128, CH], BF16) for i in range(2)]
        S.op("sp", lambda e: e.dma_start(out=Pm[:], in_=perm_d), writes=["Pm"], dma=True)
        qctr = [0]
        pend = []

        WGRP = ((0, 768), (768, 1440), (1440, 2464), (2464, 3488))

        def WK(lo):
            return [("Win", gi) for gi, (a, b_) in enumerate(WGRP) if a <= lo < b_]
        for gi, (a, b_) in enumerate(WGRP):
            S.op("pool", lambda e, a=a, b_=b_: e.dma_start(
                out=Win[:, :, a:b_], in_=w_in[:, a:b_].rearrange("(kc p) n -> p kc n", p=128)),
                writes=[("Win", gi)], dma=True)
        S.op("pool", lambda e: e.dma_start(out=Wuq[:], in_=w_uq.rearrange("(i p) n -> p i n", p=128)),
             writes=["Wuq"], dma=True)
        S.op("pool", lambda e: e.dma_start(out=Wukv[:], in_=w_ukv.rearrange("(i p) n -> p i n", p=128)),
             writes=["Wukv"], dma=True)
        S.op("sp", lambda e: e.dma_start(out=gmix[:], in_=mix_g.rearrange("(kc p) -> p kc", p=128),
                                         allow_slow_non_contiguous=True), writes=["gT"], dma=True)
        S.op("sp", lambda e: e.dma_start(out=gq[:], in_=q_g.rearrange("(kc p) -> p kc", p=128),
                                         allow_slow_non_contiguous=True), writes=["gq"], dma=True)
        S.op("sp", lambda e: e.dma_start(out=gkv[:], in_=kv_g.rearrange("(kc p) -> p kc", p=128),
                                         allow_slow_non_contiguous=True), writes=["gkv"], dma=True)
        WinK = [("Win", 0), ("Win", 1)]
        for (src_lo, n_h, half, dst_lo) in ((0, 8, 32, 0), (512, 2, 32, 512), (1408, 1, 16, 640)):
            w = n_h * 2 * half
            sv = Win[:, :, src_lo:src_lo + w].rearrange("p k (h t d) -> p k h t d", t=2, d=half)
            dv = Wsw[:, :, dst_lo:dst_lo + w].rearrange("p k (h t d) -> p k h t d", t=2, d=half)
            for t in range(2):
                S.op("pool", lambda e, sv=sv, dv=dv, t=t: e.tensor_copy(out=dv[:, :, :, t, :],
                                                                         in_=sv[:, :, :, 1 - t, :]),
                     reads=WinK, writes=["Wsw"])
        sv = Wuq[:].rearrange("p k (h f) -> p k h f", f=96)[:, :, :, 64:96].rearrange(
            "p k h (t d) -> p k h t d", t=2)
        dv = Wuqsw[:].rearrange("p k (h t d) -> p k h t d", t=2, d=16)
        for t in range(2):
            S.op("pool", lambda e, sv=sv, dv=dv, t=t: e.tensor_copy(out=dv[:, :, :, t, :],
                                                                     in_=sv[:, :, :, 1 - t, :]),
                 reads=["Wuq"], writes=["Wuqsw"])
        S.op("dve", lambda e: e.memset(sq8[:], 0.0), writes=["sq8"])
        S.op("dve", lambda e: e.memset(Va[:, :, 1, :], 1.0), writes=["Va"])
        for j in range(4):
            S.op("dve", lambda e, j=j: e.memset(Vb[:, j, :, 1, :], 1.0), writes=["Vb"])

        def norm_items(c):
            return norm_sched([norm_block(x, c * CH + j * 128, xb, junk, xs,
                                          hT[c % 2][:, :, j * 128:(j + 1) * 128], ("hT", c % 2, j), gmix)
                               for j in range(4)])

        def proj_items(c):
            items = []
            h_ = hT[c % 2]
            hK = [("hT", c % 2, j) for j in range(4)]
            tsl = slice(c * CH, (c + 1) * CH)

            def w_pairs(W, lo, hi):
                return [(W[:, kc, lo:hi], h_[:, kc, :]) for kc in range(8)]

            cA, sA, cB, sB = cA2[c % 2], sA2[c % 2], cB2[c % 2], sB2[c % 2]
            kcA, ksA, kcB, ksB = ("cA", c % 2), ("sA", c % 2), ("cB", c % 2), ("sB", c % 2)

            def load_tables():
                if c + 1 < NCH:
                    load_tables_for(c + 1)
            items.append(load_tables)

            def rope_evac(b0, b1, prt, ctab, stab, ckey, skey, out_ap, out_key):
                S.op("dve", lambda e: e.tensor_tensor(out=t1[prt, :], in0=ps[b0][prt, :], in1=ctab[prt, :],
                                                      op=ALU.mult),
                     reads=[PS(b0), ckey], writes=["t1"])
                S.op("dve", lambda e: e.tensor_tensor(out=t2[prt, :], in0=ps[b1][prt, :], in1=stab[prt, :],
                                                      op=ALU.mult),
                     reads=[PS(b1), skey], writes=["t2"])
                S.op("dve", lambda e: e.tensor_tensor(out=out_ap, in0=t1[prt, :], in1=t2[prt, :], op=ALU.add),
                     reads=["t1", "t2"], writes=[out_key])

            def rope_perm(b0, prt, nrow, ctab, stab, ckey, skey, out_ap, out_key, after=None):
                qi = qctr[0] % 2
                qctr[0] += 1
                b1 = nb()
                S.op("act", lambda e: e.activation(out=qbf[qi][prt, :], in_=ps[b0][prt, :], func=AF.Copy),
                     reads=[PS(b0)], writes=[("qbf", qi)])

                def fin():
                    S.op("pe", lambda e: e.matmul(ps[b1][prt, :], lhsT=Pm[prt, 0:nrow], rhs=qbf[qi][prt, :],
                                                  start=True, stop=True),
                         reads=[("qbf", qi), "Pm"], writes=[PS(b1)])
                    rope_evac(b0, b1, prt, ctab, stab, ckey, skey, out_ap, out_key)
                    if after is not None:
                        after()
                pend.append(fin)

            for h in range(8):
                def it(h=h):
                    b0 = nb()
                    mm_group(ps[b0][0:64, :], w_pairs(Win, h * 64, h * 64 + 64), hK + WK(0), b0)
                    rope_perm(b0, slice(0, 64), 64, cA, sA, kcA, ksA, QaT[:, h, :], "QaT")
                items.append(it)
            for g in range(2):
                def it(g=g):
                    b0 = nb()
                    mm_group(ps[b0][0:64, :], w_pairs(Win, 512 + g * 64, 512 + g * 64 + 64), hK + WK(0), b0)
                    rope_perm(b0, slice(0, 64), 64, cA, sA, kcA, ksA, KaT[:, g, :], "KaT")
                items.append(it)

            def it_va():
                b0 = nb()
                for j in range(4):
                    n = 8
                    for kc in range(8):
                        S.op("pe", lambda e, j=j, kc=kc: e.matmul(
                            ps[b0][:, j * 128:(j + 1) * 128], lhsT=h_[:, kc, j * 128:(j + 1) * 128],
                            rhs=Win[:, kc, 640:768], start=(kc == 0), stop=(kc == 7)),
                            reads=hK + WK(0), writes=[PS(b0)])
                S.op("act", lambda e: e.activation(
                    out=Va[:, :, 0:3:2, :],
                    in_=ps[b0][:].rearrange("p (j g d) -> p j g d", j=4, g=2), func=AF.Copy),
                    reads=[PS(b0)], writes=["Va"])
                S.op("sp", lambda e: e.dma_start(out=Vad[:, 4 * c:4 * c + 4, :, :], in_=Va[:]),
                     reads=["Va"], writes=[("Vad", c)], dma=True)
            items.append(it_va)

            def latent(lo, ntile, gvec, gkey, outT, okey, dim):
                def it():
                    banks = []
                    for i in range(ntile):
                        b0 = nb()
                        banks.append(b0)
                        mm_group(ps[b0][:, :], w_pairs(Win, lo + i * 128, lo + (i + 1) * 128), hK + WK(lo), b0)
                        S.op("act", lambda e, i=i, b0=b0: e.activation(out=sq[:, i, :], in_=ps[b0][:, :],
                                                                        func=AF.Square),
                             reads=[PS(b0)], writes=[("sq", i)])
                        S.op("dve", lambda e, i=i, b0=b0: e.tensor_copy(out=qlf[:, i, :], in_=ps[b0][:, :]),
                             reads=[PS(b0)], writes=[("qlf", i)])
                    bs = nb()
                    mm_group(ps[bs][:, :], [(ones_bf[:, :], sq[:, i, :]) for i in range(ntile)],
                             [("sq", i) for i in range(ntile)] + ["ones_bf"], bs)
                    S.op("act", lambda e: e.activation(out=rq[:], in_=ps[bs][:, :], func=AF.Ln,
                                                       scale=1.0 / dim, bias=EPS),
                         reads=[PS(bs)], writes=["rq"])
                    S.op("act", lambda e: e.activation(out=rq[:], in_=rq[:], func=AF.Exp, scale=-0.5),
                         reads=["rq"], writes=["rq"])
                    for i in range(ntile):
                        S.op("dve", lambda e, i=i: e.scalar_tensor_tensor(
                            out=outT[:, i, :], in0=qlf[:, i, :], scalar=gvec[:, i:i + 1], in1=rq[:],
                            op0=ALU.mult, op1=ALU.mult),
                            reads=[("qlf", i), "rq", gkey], writes=[okey])
                return it
            items.append(latent(768, 3, gq, "gq", cqT, "cqT", 384.0))
            items.append(latent(1152, 2, gkv, "gkv", ckvT, "ckvT", 256.0))

            def it_kr():
                b0 = nb()
                mm_group(ps[b0][64:96, :], w_pairs(Win, 1408, 1440), hK + WK(1408), b0)

                def bcast():
                    S.op("act", lambda e: e.activation(
                        out=KT[64:96, :, :], in_=t1[64:96, :].unsqueeze(1).to_broadcast([32, 8, CH]),
                        func=AF.Copy),
                        reads=["t1"], writes=["KT"])
                rope_perm(b0, slice(64, 96), 32, cB, sB, kcB, ksB, t1[64:96, :], "t1", after=bcast)
            items.append(it_kr)

            for h in range(8):
                def it(h=h):
                    b0 = nb()
                    cq_pairs = lambda W, lo, hi: [(W[:, i, lo:hi], cqT[:, i, :]) for i in range(3)]
                    mm_group(ps[b0][0:64, :], cq_pairs(Wuq, h * 96, h * 96 + 64), ["cqT", "Wuq"], b0)
                    mm_group(ps[b0][64:96, :], cq_pairs(Wuq, h * 96 + 64, h * 96 + 96), ["cqT", "Wuq"], b0)
                    S.op("act", lambda e: e.activation(out=QT[0:64, h, :], in_=ps[b0][0:64, :], func=AF.Copy),
                         reads=[PS(b0)], writes=["QT"])
                    rope_perm(b0, slice(64, 96), 32, cB, sB, kcB, ksB, QT[64:96, h, :], "QT")
                items.append(it)

            for h in range(8):
                def it(h=h):
                    b0 = nb()
                    mm_group(ps[b0][0:64, :], [(Wukv[:, i, h * 128:h * 128 + 64], ckvT[:, i, :]) for i in range(2)],
                             ["ckvT", "Wukv"], b0)
                    S.op("act", lambda e: e.activation(out=KT[0:64, h, :], in_=ps[b0][0:64, :], func=AF.Copy),
                         reads=[PS(b0)], writes=["KT"])
                items.append(it)
            for j in range(4):
                def it(j=j):
                    b0 = nb()
                    wv = Wukv[:].rearrange("p k (h f) -> p k h f", f=128)
                    mm_group(ps[b0][:, :], [(ckvT[:, i, j * 128:(j + 1) * 128], wv[:, i, :, 64:128]) for i in range(2)],
                             ["ckvT", "Wukv"], b0)
                    pv4 = ps[b0][:].rearrange("p (pr par d) -> p pr par d", pr=4, par=2)
                    for par in range(2):
                        S.op("dve", lambda e, par=par: e.tensor_copy(
                            out=Vb[:, j, :, 2 * par, :], in_=pv4[:, :, par, :]),
                            reads=[PS(b0)], writes=["Vb"])
                items.append(it)

            for m in range(16):
                def it(m=m):
                    b0 = nb()
                    mm_group(ps[b0][:, :], w_pairs(Win, 1440 + m * 128, 1440 + (m + 1) * 128), hK + WK(1440 + m * 128), b0)
                    G = Ga if m < 8 else Gb
                    S.op("act", lambda e: e.activation(out=G[:, m % 8, :], in_=ps[b0][:, :], func=AF.Sigmoid),
                         reads=[PS(b0)], writes=["Ga" if m < 8 else "Gb"])
                items.append(it)

            def stats(T, tkey, nrow, nh, stat, col0):
                def prep():
                    S.op("act", lambda e: e.activation(out=sq8[0:nrow, 0:nh, :], in_=T[0:nrow, 0:nh, :],
                                                       func=AF.Square),
                         reads=[tkey], writes=["sq8"])

                def pe_():
                    for h in range(nh):
                        b0 = nb()
                        mm_group(ps[b0][:, :], [(ones_bf[0:nrow, :], sq8[0:nrow, h, :])], ["sq8", "ones_bf"], b0)
                        S.op("dve", lambda e, h=h, b0=b0: e.tensor_reduce(
                            out=stat[:, col0 + h:col0 + h + 1], in_=ps[b0][:, :], axis=AX.X, op=ALU.max),
                            reads=[PS(b0)], writes=[("stat", id(stat), col0 + h)])
                return prep, pe_
            stA = stats(QaT, "QaT", 64, 8, statAq, c * 8)
            stB = stats(KaT, "KaT", 64, 2, statAk, c * 2)
            stC = stats(QT, "QT", 96, 8, statBq, c * 8)
            stD = stats(KT, "KT", 96, 8, statBk, c * 8)

            def spill():
                for (dst, src, k, dk) in ((QaTd[:, :, tsl], QaT, "QaT", "QaTd"),
                                          (KaTd[:, :, tsl], KaT, "KaT", "KaTd"),
                                          (QTd[:, :, tsl], QT, "QT", "QTd"),
                                          (KTd[:, :, tsl], KT, "KT", "KTd"),
                                          (Gad[:, :, tsl], Ga, "Ga", "Gad"),
                                          (Gbd[:, :, tsl], Gb, "Gb", "Gbd")):
                    S.op("sp", lambda e, dst=dst, src=src: e.dma_start(out=dst, in_=src[:]),
                         reads=[k], writes=[(dk, c)], dma=True)
                for j in range(4):
                    S.op("dve", lambda e, j=j: e.memset(Vb[:, j, :, 1, :], 1.0), writes=["Vb"])
                S.op("sp", lambda e: e.dma_start(out=Vd[:, 4 * c:4 * c + 4, :, :, :], in_=Vb[:]),
                     reads=["Vb"], writes=[("Vd", c)], dma=True)
            (tab, swaq, swak, va_, latq, latkv, kr_, mlaq, knope, vmla, gates) = (
                items[0:1], items[1:9], items[9:11], items[11:12], items[12:13], items[13:14], items[14:15],
                items[15:23], items[23:31], items[31:35], items[35:51])
            assert len(items) == 51, len(items)
            out_items = (tab + latq + latkv + swaq[0:4] + kr_ + swaq[4:8] + swak + va_ + mlaq + knope + vmla
                         + gates[0:3] + [stA[0]] + gates[3:6] + [stA[1], stB[0]] + gates[6:8] + [stB[1], stC[0]]
                         + gates[8:12] + [stC[1], stD[0]] + gates[12:16] + [stD[1], spill])

            def wrap(it):
                def w():
                    old = list(pend)
                    del pend[:]
                    it()
                    for f in old:
                        f()
                return w
            return [wrap(it) for it in out_items]

        load_tables_for(0)
        for it in norm_items(0):
            it()
        for c in range(NCH):
            nxt = norm_items(c + 1) if c + 1 < NCH else []
            for it in _interleave(proj_items(c), nxt):
                it()
        S.barrier()
        st.close()

    if 2 in phases:
        st = contextlib.ExitStack()
        KTc = sbt(st, "KTc", [128, 8, SEQ], BF16)
        Vc = sbt(st, "Vc", [128, 32, 4, 192], BF16)
        Wosa = sbt(st, "Wosa", [128, 4, DM], BF16)
        Wosb = sbt(st, "Wosb", [128, 4, DM], BF16)
        Wout = sbt(st, "Wout", [128, 8, DM], BF16)
        QaT_2 = sbt(st, "QaT2", [64, 8, CH], BF16)
        QT_2 = sbt(st, "QT2", [128, 8, CH], BF16)
        Gt = [sbt(st, "Gt%d" % i, [128, 2, CH], BF16) for i in range(3)]
        Ka5 = sbt(st, "Ka5", [64, 2, 5 * 128], BF16)
        Va5 = sbt(st, "Va5", [128, 5, 192], BF16)
        OaT = sbt(st, "OaT2", [128, 4, CH], BF16)
        ObT = sbt(st, "ObT2", [128, 4, CH], BF16)
        yT = sbt(st, "yT", [128, 8, CH], BF16)
        pt = [sbt(st, "pt%d" % i, [128, 2, CH], BF16) for i in range(3)]
        rl = sbt(st, "rl", [128, CH], F32)
        xh = [sbt(st, "xh%d" % i, [128, CH], F32) for i in range(3)]
        ty1 = xh[0]
        mdiag = sbt(st, "mdiag", [128, 4, 128], BF16)
        mprev = sbt(st, "mprev", [128, 4, 128], BF16)
        e128 = sbt(st, "e128", [1, 2, 128], BF16)
        sinkP = sbt(st, "sinkP", [1, 8 * 128], BF16)
        sinkB = sbt(st, "sinkB", [128, 8], F32)
        negc = sbt(st, "negc", [128, 4], F32)
        ty2 = xh[1]

        for (dst, src, k) in ((mdiag, mdiag_d, "mdiag"), (mprev, mprev_d, "mprev"), (e128, e128_d, "e128"),
                              (sinkB, sinkb, "sinkB")):
            S.op("sp", lambda e, dst=dst, src=src: e.dma_start(out=dst[:], in_=src), writes=[k], dma=True)
        S.op("sp", lambda e: e.dma_start(out=xh[0][0:1, :], in_=sinkrep[:, 0:512]), writes=[("xh", 0)], dma=True)
        S.op("sp", lambda e: e.dma_start(out=xh[1][0:1, :], in_=sinkrep[:, 512:1024]), writes=[("xh", 1)], dma=True)
        for g in range(2):
            S.op("pool", lambda e, g=g: e.dma_start(
                out=Wosa[g * 64:(g + 1) * 64, :, :],
                in_=w_osa[g * 256:(g + 1) * 256, :].rearrange("(hh d) n -> d hh n", d=64)),
                writes=["Wosa"], dma=True)
        S.op("pool", lambda e: e.dma_start(out=Wosb[:], in_=w_osb.rearrange("(pr p) n -> p pr n", p=128)),
             writes=["Wosb"], dma=True)
        S.op("pool", lambda e: e.dma_start(out=Wout[:], in_=w_out.rearrange("(kc p) n -> p kc n", p=128)),
             writes=["Wout"], dma=True)

        def nop_(fn, r=(), w=("negc",)):
            S.op("dve", fn, reads=list(r) + ["negc"], writes=list(w))
        nop_(lambda e: e.tensor_reduce(out=negc[:, 2:3], in_=statAq[:], axis=AX.X, op=ALU.max))
        nop_(lambda e: e.tensor_reduce(out=negc[:, 3:4], in_=statAk[:], axis=AX.X, op=ALU.max))
        nop_(lambda e: e.tensor_tensor(out=negc[:, 0:1], in0=negc[:, 2:3], in1=negc[:, 3:4], op=ALU.max))
        nop_(lambda e: e.tensor_reduce(out=negc[:, 2:3], in_=sinkB[:], axis=AX.X, op=ALU.max), r=["sinkB"])
        nop_(lambda e: e.scalar_tensor_tensor(out=negc[:, 0:1], in0=negc[:, 0:1], scalar=SCALE_A,
                                              in1=negc[:, 2:3], op0=ALU.mult, op1=ALU.max))
        nop_(lambda e: e.tensor_scalar(out=negc[:, 0:1], in0=negc[:, 0:1], scalar1=-1.0, scalar2=None,
                                       op0=ALU.mult))
        nop_(lambda e: e.tensor_reduce(out=negc[:, 2:3], in_=statBq[:], axis=AX.X, op=ALU.max))
        nop_(lambda e: e.tensor_reduce(out=negc[:, 3:4], in_=statBk[:], axis=AX.X, op=ALU.max))
        nop_(lambda e: e.tensor_tensor(out=negc[:, 1:2], in0=negc[:, 2:3], in1=negc[:, 3:4], op=ALU.max))
        nop_(lambda e: e.tensor_scalar(out=negc[:, 1:2], in0=negc[:, 1:2], scalar1=-SCALE_B, scalar2=None,
                                       op0=ALU.mult))
        for hf in range(2):
            S.op("act", lambda e, hf=hf: e.activation(out=sinkP[:, hf * 512:(hf + 1) * 512], in_=xh[hf][0:1, :],
                                                      func=AF.Exp, bias=negc[0:1, 0:1]),
                 reads=["negc", ("xh", hf)], writes=["sinkP"])

        brow = sbt(st, "brow", [1, CH], BF16)
        negm = sbt(st, "negm", [128, 1], F32)
        for hh_ in range(2):
            S.op("sp", lambda e, hh_=hh_: e.dma_start(out=KTc[96:97, 4 * hh_:4 * hh_ + 4, :],
                                                      in_=onesrow_d[:, 4 * hh_:4 * hh_ + 4, :]),
                 writes=[("KTc", cc) for cc in range(NCH)], dma=True)
        S.op("dve", lambda e: e.tensor_scalar(out=negm[:], in0=negc[:, 1:2], scalar1=1.0 / SCALE_B, scalar2=None,
                                              op0=ALU.mult),
             reads=["negc"], writes=["negm"])
        S.op("dve", lambda e: e.tensor_scalar(out=brow[:], in0=ones_bf[0:1, 0:1].to_broadcast([1, CH]),
                                              scalar1=negm[0:1, 0:1], scalar2=None, op0=ALU.mult),
             reads=["negm", "ones_bf"], writes=["brow"])
        S.op("sp", lambda e: e.dma_start(out=QT_2[96:97, :, :],
                                         in_=brow[0:1, :].unsqueeze(1).to_broadcast([1, 8, CH])),
             reads=["brow"], writes=["QT_2"], dma=True)

        OB = (6, 7)
        octr = [0]

        def load_chunk(c, q, q_qt=None):
            q_qt = q_qt or q
            tsl = slice(c * CH, (c + 1) * CH)
            S.op(q, lambda e: e.dma_start(out=QaT_2[:], in_=QaTd[:, :, tsl]),
                 reads=[("QaTd", c)], writes=["QaT_2"], dma=True)
            if c == 0:
                S.op(q, lambda e: e.dma_start(out=Ka5[:, :, 128:640], in_=KaTd[:, :, 0:512]),
                     reads=[("KaTd", 0)], writes=["Ka5"], dma=True)
                S.op(q, lambda e: e.dma_start(out=Va5[:, 1:5, :], in_=Vad[:, 0:4, :, :].rearrange("p b s d -> p b (s d)")),
                     reads=[("Vad", 0)], writes=["Va5"], dma=True)
            else:
                S.op(q, lambda e: e.dma_start(out=Ka5[:], in_=KaTd[:, :, c * CH - 128:(c + 1) * CH]),
                     reads=[("KaTd", c), ("KaTd", c - 1)], writes=["Ka5"], dma=True)
                S.op(q, lambda e: e.dma_start(out=Va5[:], in_=Vad[:, 4 * c - 1:4 * c + 4, :, :].rearrange("p b s d -> p b (s d)")),
                     reads=[("Vad", c), ("Vad", c - 1)], writes=["Va5"], dma=True)
            S.op(q, lambda e: e.dma_start(out=KTc[0:96, :, tsl], in_=KTd[:, :, tsl]),
                 reads=[("KTd", c)], writes=[("KTc", c)], dma=True)
            S.op(q, lambda e: e.dma_start(out=Vc[:, 4 * c:4 * c + 4, :, :],
                                          in_=Vd[:, 4 * c:4 * c + 4, :, :, :].rearrange("p b r s d -> p b r (s d)")),
                 reads=[("Vd", c)], writes=[("Vc", c)], dma=True)
            S.op(q_qt, lambda e: e.dma_start(out=QT_2[0:96, :, :], in_=QTd[:, :, tsl]),
                 reads=[("QTd", c)], writes=["QT_2"], dma=True)

        def load_gt(c, m):
            tsl = slice(c * CH, (c + 1) * CH)
            S.op("sp", lambda e: e.dma_start(out=Gt[m % 3][:, 0, :], in_=Gad[:, m, tsl]),
                 reads=[("Gad", c)], writes=[("Gt", m % 3)], dma=True)
            S.op("sp", lambda e: e.dma_start(out=Gt[m % 3][:, 1, :], in_=Gbd[:, m, tsl]),
                 reads=[("Gbd", c)], writes=[("Gt", m % 3)], dma=True)

        def normalize(ob, o_lo, outT, okey, split4, on_act=False):
            l_lo = 64 - o_lo
            lk = ("rl", l_lo)
            if on_act:
                S.op("act", lambda e: e.activation(out=rl[l_lo:l_lo + 64, :], in_=ps[ob][l_lo:l_lo + 64, :],
                                                   func=AF.Ln),
                     reads=[PS(ob)], writes=[lk])
                S.op("act", lambda e: e.activation(out=rl[l_lo:l_lo + 64, :], in_=rl[l_lo:l_lo + 64, :],
                                                   func=AF.Exp, scale=-1.0),
                     reads=[lk], writes=[lk])
            else:
                S.op("dve", lambda e: e.reciprocal(out=rl[l_lo:l_lo + 64, :], in_=ps[ob][l_lo:l_lo + 64, :]),
                     reads=[PS(ob)], writes=[lk])
            a0 = ps[ob][o_lo:o_lo + 64, :]
            a1 = rl[l_lo:l_lo + 64, :]
            if split4:
                a0 = a0.rearrange("p (h q) -> p h q", h=4)
                a1 = a1.rearrange("p (h q) -> p h q", h=4)
            S.op("dve", lambda e: e.tensor_tensor(out=outT, in0=a0, in1=a1, op=ALU.mult),
                 reads=[PS(ob), lk], writes=[okey])

        def PP(P):
            return [PS(2 * P), PS(2 * P + 1)]

        def swa_jobs(c):
            jobs = []
            for g in range(2):
                for j in range(4):
                    subs = [(j, mprev, "mprev")] if (c > 0 or j > 0) else []
                    subs.append((j + 1, mdiag, "mdiag"))

                    def mk(g=g, j=j, subs=subs):
                        ob = OB[octr[0] % 2]
                        octr[0] += 1
                        qv = QaT_2[0:64, 4 * g:4 * g + 4, j * 128:(j + 1) * 128]
                        ns = len(subs)

                        def qk(P):
                            for si, (kb5, mask, mk_) in enumerate(subs):
                                S.op("pe", lambda e, si=si, kb5=kb5: e.matmul(
                                    pp[P][:, si, :], lhsT=Ka5[:, g, kb5 * 128:(kb5 + 1) * 128], rhs=qv,
                                    start=True, stop=False),
                                    reads=["Ka5", "QaT_2"], writes=[PS(2 * P + si)])
                                S.op("pe", lambda e, si=si, mask=mask: e.matmul(
                                    pp[P][:, si, :], lhsT=ident[:], rhs=mask[:], start=False, stop=True),
                                    reads=["ident", mk_], writes=[PS(2 * P + si)])

                        def ex(P):
                            S.op("act", lambda e: e.activation(out=pt[P][:, 0:ns, :], in_=pp[P][:, 0:ns, :],
                                                               func=AF.Exp, scale=SCALE_A, bias=negc[:, 0:1]),
                                 reads=PP(P)[0:ns] + ["negc"], writes=[("pt", P)])

                        def pv(P):
                            for si, (kb5, mask, mk_) in enumerate(subs):
                                S.op("pe", lambda e, si=si, kb5=kb5: e.matmul(
                                    ps[ob][:, :], lhsT=Va5[:, kb5, g * 64:g * 64 + 128], rhs=pt[P][:, si, :],
                                    start=(si == 0), stop=False),
                                    reads=["Va5", ("pt", P)], writes=[PS(ob)])
                            S.op("pe", lambda e: e.matmul(
                                ps[ob][:, :], lhsT=e128[0:1, g, :], rhs=sinkP[0:1, 4 * g * 128:(4 * g + 4) * 128],
                                start=False, stop=True),
                                reads=["e128", "sinkP"], writes=[PS(ob)])
                            normalize(ob, 64 * g, OaT[64 * g:64 * g + 64, :, j * 128:(j + 1) * 128], "OaT", True,
                                      on_act=(j % 2 == 0))
                        return dict(qk=qk, ex=ex, pv=pv)
                    jobs.append(mk())
            return jobs

        def mla_jobs(c):
            jobs = []
            nk = 4 * c + 4
            KTk = [("KTc", cc) for cc in range(c + 1)]
            Vk = [("Vc", cc) for cc in range(c + 1)]
            for h in range(8):
                ob = OB[octr[0] % 2]
                octr[0] += 1
                pr, par = h // 2, h % 2
                for t in range(nk // 2):
                    kbs = (2 * t, 2 * t + 1)
                    diag = kbs[0] >= 4 * c

                    def mk(h=h, ob=ob, pr=pr, par=par, kbs=kbs, diag=diag):
                        def geom(kb):
                            j = kb - 4 * c
                            q0 = max(j, 0) * 128
                            return q0, CH - q0

                        def qk(P):
                            for si, kb in enumerate(kbs):
                                q0, n = geom(kb)
                                S.op("pe", lambda e, si=si, kb=kb, q0=q0, n=n: e.matmul(
                                    pp[P][:, si, 0:n], lhsT=KTc[0:97, h, kb * 128:(kb + 1) * 128],
                                    rhs=QT_2[0:97, h, q0:CH], start=True, stop=(not diag)),
                                    reads=KTk + ["QT_2"], writes=[PS(2 * P + si)])
                                if diag:
                                    S.op("pe", lambda e, si=si: e.matmul(
                                        pp[P][:, si, 0:128], lhsT=ident[:], rhs=mdiag[:, 0, :],
                                        start=False, stop=True),
                                        reads=["ident", "mdiag"], writes=[PS(2 * P + si)])

                        def ex(P):
                            if not diag:
                                S.op("act", lambda e: e.activation(out=pt[P][:], in_=pp[P][:], func=AF.Exp,
                                                                   scale=SCALE_B),
                                     reads=PP(P), writes=[("pt", P)])
                            else:
                                for si, kb in enumerate(kbs):
                                    q0, n = geom(kb)
                                    S.op("act", lambda e, si=si, n=n: e.activation(
                                        out=pt[P][:, si, 0:n], in_=pp[P][:, si, 0:n], func=AF.Exp,
                                        scale=SCALE_B),
                                        reads=[PS(2 * P + si)], writes=[("pt", P)])

                        def pv(P):
                            for si, kb in enumerate(kbs):
                                q0, n = geom(kb)
                                last = (kb == nk - 1)
                                S.op("pe", lambda e, si=si, kb=kb, q0=q0, n=n, last=last: e.matmul(
                                    ps[ob][:, q0:CH], lhsT=Vc[:, kb, pr, par * 64:par * 64 + 128],
                                    rhs=pt[P][:, si, 0:n], start=(kb == 0), stop=last),
                                    reads=Vk + [("pt", P)], writes=[PS(ob)])
                            if kbs[1] == nk - 1:
                                normalize(ob, 64 * par, ObT[64 * par:64 * par + 64, pr, :], "ObT", False,
                                          on_act=(c <= 1))
                        return dict(qk=qk, ex=ex, pv=pv)
                    jobs.append(mk())
            return jobs

        def outproj_jobs(c):
            jobs = []
            for j in range(4):
                for n_ in range(2):
                    def mk(j=j, n_=n_):
                        s_ = 2 * j + n_
                        hb = s_ % 3
                        t0 = c * CH + j * 128
                        nsl = slice(n_ * 512, (n_ + 1) * 512)

                        def qk(P):
                            S.op("sp", lambda e: e.dma_start(out=xh[hb][:], in_=x[t0:t0 + 128, nsl]),
                                 writes=[("xh", hb)], dma=True)
                            mm_group(pp[P][:, 0, :], [(yT[:, kc, j * 128:(j + 1) * 128], Wout[:, kc, nsl])
                                                      for kc in range(8)], ["yT", "Wout"], 2 * P)

                        def ex(P):
                            S.op("dve", lambda e: e.tensor_tensor(out=xh[hb][:], in0=pp[P][:, 0, :], in1=xh[hb][:],
                                                                  op=ALU.add),
                                 reads=[PS(2 * P), ("xh", hb)], writes=[("xh", hb)])
                            S.op("sp", lambda e: e.dma_start(out=x1d[t0:t0 + 128, nsl], in_=xh[hb][:]),
                                 reads=[("xh", hb)], writes=[("x1d", c, j, n_)], dma=True)

                        def pv(P):
                            pass
                        return dict(qk=qk, ex=ex, pv=pv)
                    jobs.append(mk())
            return jobs

        LOOK = 2

        def run_jobs(jobs, pctr):
            n = len(jobs)
            base = pctr[0]
            for k in range(n + LOOK):
                if k < n:
                    jobs[k]["qk"]((base + k) % 3)
                if k - LOOK >= 0:
                    P = (base + k - LOOK) % 3
                    jobs[k - LOOK]["ex"](P)
                    jobs[k - LOOK]["pv"](P)
            pctr[0] = base + n

        pctr = [0]
        load_chunk(0, "sp")
        for m in range(3):
            load_gt(0, m)

        def chunk2(c):
            mj = mla_jobs(c)
            if c > 0:
                mj = _interleave(mj, outproj_jobs(c - 1))
            run_jobs(swa_jobs(c) + mj, pctr)
            if c + 1 < NCH:
                load_chunk(c + 1, "sp", q_qt="pool")

            for m in range(8):
                P = m % 3
                msl = slice(m * 128, (m + 1) * 128)
                mm_group(pp[P][:, 0, :], [(Wosa[:, hh, msl], OaT[:, hh, :]) for hh in range(4)],
                         ["Wosa", "OaT"], 2 * P)
                mm_group(pp[P][:, 1, :], [(Wosb[:, pr, msl], ObT[:, pr, :]) for pr in range(4)],
                         ["Wosb", "ObT"], 2 * P + 1)
                S.op("dve", lambda e, P=P, m=m: e.tensor_tensor(out=ty1[:], in0=pp[P][:, 0, :], in1=Gt[m % 3][:, 0, :],
                                                                op=ALU.mult),
                     reads=[PS(2 * P), ("Gt", m % 3)], writes=[("xh", 0)])
                S.op("dve", lambda e, P=P, m=m: e.tensor_tensor(out=ty2[:], in0=pp[P][:, 1, :], in1=Gt[m % 3][:, 1, :],
                                                                op=ALU.mult),
                     reads=[PS(2 * P + 1), ("Gt", m % 3)], writes=[("xh", 1)])
                S.op("dve", lambda e, m=m: e.tensor_tensor(out=yT[:, m, :], in0=ty1[:], in1=ty2[:], op=ALU.add),
                     reads=[("xh", 0), ("xh", 1)], writes=["yT"])
                if m + 3 < 8:
                    load_gt(c, m + 3)
            if c + 1 < NCH:
                for m in range(3):
                    load_gt(c + 1, m)

            if debug and c == 0:
                S.op("sp", lambda e: e.dma_start(out=dOaT, in_=OaT[:]), reads=["OaT"], dma=True)
                S.op("sp", lambda e: e.dma_start(out=dObT, in_=ObT[:]), reads=["ObT"], dma=True)
                S.op("sp", lambda e: e.dma_start(out=dyT, in_=yT[:]), reads=["yT"], dma=True)
                S.op("sp", lambda e: e.dma_start(out=dVc, in_=Vc[:, 0:4, :, :]), reads=[("Vc", 0)], dma=True)
                S.op("sp", lambda e: e.dma_start(out=dKT, in_=KTc[:, :, 0:CH]), reads=[("KTc", 0)], dma=True)
                S.op("sp", lambda e: e.dma_start(out=dQT, in_=QT_2[:]), reads=["QT_2"], dma=True)
            if c == NCH - 1:
                run_jobs(outproj_jobs(c), pctr)

        for c in range(NCH):
            chunk2(c)
        S.barrier()
        st.close()

    if 3 in phases:
        st = contextlib.ExitStack()
        Wg = sbt(st, "Wg", [128, 8, DFF], BF16)
        Wu = sbt(st, "Wu", [128, 8, DFF], BF16)
        Wd = sbt(st, "Wd", [128, NFF, DM], BF16)
        gffn = sbt(st, "gffn", [128, 8], F32)
        gfin = sbt(st, "gfin", [128, DM], F32)
        xb = [sbt(st, "fxb%d" % i, [128, DM], F32) for i in range(2)]
        junk = sbt(st, "fjunk", [128, DM], BF16)
        xs = [sbt(st, "fxs%d" % i, [128, DM], BF16) for i in range(2)]
        hT = [sbt(st, "fhT%d" % i, [128, 8, CH], BF16) for i in range(2)]
        actT = sbt(st, "actT", [128, NFF, CH], BF16)
        sg = [sbt(st, "sg%d" % i, [128, CH], F32) for i in range(2)]
        xr = sbt(st, "xr", [128, DM], F32)
        x2 = sbt(st, "x2", [128, DM], F32)
        ob_ = sbt(st, "ob_", [128, DM], F32)
        junk2 = sbt(st, "junk2", [128, DM], BF16)
        ss2 = sbt(st, "ss2", [128, 1], F32)
        rs2 = sbt(st, "rs2", [128, 1], F32)

        FG = ((0, 768), (768, 1536), (1536, 2176), (2176, DFF))

        def FGRP(m):
            return [gi for gi, (a, b_) in enumerate(FG) if a <= m * 128 < b_][0]
        for gi, (a, b_) in enumerate(FG):
            S.op("pool", lambda e, a=a, b_=b_: e.dma_start(
                out=Wg[:, :, a:b_], in_=w_gate[:, a:b_].rearrange("(kc p) n -> p kc n", p=128)),
                writes=[("Wgu", gi)], dma=True)
            S.op("pool", lambda e, a=a, b_=b_: e.dma_start(
                out=Wu[:, :, a:b_], in_=w_up[:, a:b_].rearrange("(kc p) n -> p kc n", p=128)),
                writes=[("Wgu", gi)], dma=True)
        for (m0, m1) in ((0, 6), (6, 12), (12, 17), (17, NFF)):
            S.op("pool", lambda e, m0=m0, m1=m1: e.dma_start(
                out=Wd[:, m0:m1, :], in_=w_down[m0 * 128:m1 * 128, :].rearrange("(m p) n -> p m n", p=128)),
                writes=["Wd"], dma=True)
        S.op("sp", lambda e: e.dma_start(out=gffn[:], in_=ffn_g.rearrange("(kc p) -> p kc", p=128),
                                         allow_slow_non_contiguous=True), writes=["gT"], dma=True)
        S.op("sp", lambda e: e.dma_start(out=gfin[:], in_=fin_g.partition_broadcast(128)),
             writes=["gfin"], dma=True)

        def ffn_norm_items(c):
            return norm_sched([norm_block(x1d, c * CH + j * 128, xb, junk, xs,
                                          hT[c % 2][:, :, j * 128:(j + 1) * 128], ("fhT", c % 2, j), gffn)
                               for j in range(4)])

        def ffn_items(c):
            items = []
            hK = [("fhT", c % 2, j) for j in range(4)]
            h_ = hT[c % 2]
            for m in range(NFF):
                def it(m=m):
                    bg, bu = nb(), nb()
                    msl = slice(m * 128, (m + 1) * 128)
                    wk = [("Wgu", FGRP(m))]
                    mm_group(ps[bg][:, :], [(Wg[:, kc, msl], h_[:, kc, :]) for kc in range(8)], hK + wk, bg)
                    mm_group(ps[bu][:, :], [(Wu[:, kc, msl], h_[:, kc, :]) for kc in range(8)], hK + wk, bu)
                    s_ = m % 2
                    S.op("act", lambda e: e.activation(out=sg[s_][:], in_=ps[bg][:, :], func=AF.Silu),
                         reads=[PS(bg)], writes=[("sg", s_)])
                    S.op("dve", lambda e: e.tensor_tensor(out=actT[:, m, :], in0=ps[bu][:, :],
                                                          in1=sg[s_][:], op=ALU.mult),
                         reads=[PS(bu), ("sg", s_)], writes=[("actT", m)])
                items.append(it)
            aK = [("actT", m) for m in range(NFF)]
            for j in range(4):
                def it(j=j):
                    t0 = c * CH + j * 128
                    S.op("sp", lambda e: e.dma_start(out=xr[:], in_=x1d[t0:t0 + 128, :]),
                         reads=[("x1d", c, j, 0), ("x1d", c, j, 1)], writes=["xr"], dma=True)
                    for n_ in range(2):
                        bo = nb()
                        mm_group(ps[bo][:, :], [(actT[:, m, j * 128:(j + 1) * 128], Wd[:, m, n_ * 512:(n_ + 1) * 512])
                                                for m in range(NFF)], aK + ["Wd"], bo)
                        S.op("dve", lambda e, bo=bo, n_=n_: e.tensor_tensor(
                            out=x2[:, n_ * 512:(n_ + 1) * 512], in0=ps[bo][:, :], in1=xr[:, n_ * 512:(n_ + 1) * 512],
                            op=ALU.add),
                            reads=[PS(bo), "xr"], writes=["x2"])
                    S.op("act", lambda e: e.activation(out=junk2[:], in_=x2[:], func=AF.Square, accum_out=ss2[:]),
                         reads=["x2"], writes=["junk2", "ss2"])
                    S.op("act", lambda e: e.activation(out=rs2[:], in_=ss2[:], func=AF.Ln, scale=1.0 / DM, bias=EPS),
                         reads=["ss2"], writes=["rs2"])
                    S.op("act", lambda e: e.activation(out=rs2[:], in_=rs2[:], func=AF.Exp, scale=-0.5),
                         reads=["rs2"], writes=["rs2"])
                    S.op("dve", lambda e: e.scalar_tensor_tensor(out=ob_[:], in0=x2[:], scalar=rs2[:, 0:1], in1=gfin[:],
                                                                 op0=ALU.mult, op1=ALU.mult),
                         reads=["x2", "rs2", "gfin"], writes=["ob_"])
                    S.op("sp", lambda e: e.dma_start(out=out[t0:t0 + 128, :], in_=ob_[:]),
                         reads=["ob_"], dma=True)
                items.append(it)
            return items

        for it in ffn_norm_items(0):
            it()
        for c in range(NCH):
            nxt = ffn_norm_items(c + 1) if c + 1 < NCH else []
            for it in _interleave(ffn_items(c), nxt):
                it()
        st.close()

    S.emit()
    top.close()
    return nc


def _consts():
    bf = ml_dtypes.bfloat16
    pos = np.arange(SEQ, dtype=np.float32)[:, None]

    def tables(dim):
        inv = (10000.0 ** (-np.arange(0, dim, 2, dtype=np.float32) / dim)).astype(np.float32)
        ang = (pos * inv[None, :]).astype(np.float32)
        c = np.cos(ang).astype(np.float32).T
        s = np.sin(ang).astype(np.float32).T
        return (np.ascontiguousarray(np.concatenate([c, c], 0)),
                np.ascontiguousarray(np.concatenate([-s, s], 0)))

    cosA, sinA = tables(64)
    cosB, sinB = tables(32)
    k = np.arange(128)[:, None]
    q = np.arange(128)[None, :]
    mdiag = np.where(k <= q, 0.0, NEG).astype(np.float32)
    mprev = np.where(k > q, 0.0, NEG).astype(np.float32)
    e128 = np.zeros((1, 2, 128), np.float32)
    e128[0, 0, 64:] = 1.0
    e128[0, 1, :64] = 1.0
    return {
        "cosA": cosA, "sinA": sinA, "cosB": cosB, "sinB": sinB,
        "ident": np.eye(128, dtype=np.float32).astype(bf),
        "mdiag4": np.ascontiguousarray(np.broadcast_to(mdiag[:, None, :], (128, 4, 128))).astype(bf),
        "mprev4": np.ascontiguousarray(np.broadcast_to(mprev[:, None, :], (128, 4, 128))).astype(bf),
        "e128": e128.astype(bf),
        "ropeperm": _ropeperm().astype(bf),
        "onesrow": np.ones((1, 8, SEQ), np.float32).astype(bf),
    }


def _ropeperm():
    p = np.zeros((128, 64), np.float32)
    for m in range(64):
        p[(m + 32) % 64, m] = 1.0
    for m in range(32):
        p[64 + (m + 16) % 32, m] = 1.0
    return p


_NC_CACHE = {}


def kernel(x, mix_norm_g, w_in, swa_sinks, q_norm_g, w_uq, kv_norm_g, w_ukv,
           w_o_swa, w_o_mla, w_out, ffn_norm_g, w_gate, w_up, w_down, final_norm_g,
           _debug=False, _phases=(1, 2, 3)):
    f = lambda a: np.ascontiguousarray(np.asarray(a, dtype=np.float32))
    x = f(x)
    sinks = f(swa_sinks)[0]
    shared = {
        "mix_norm_g": f(mix_norm_g)[0], "w_in": f(w_in)[0], "q_norm_g": f(q_norm_g)[0],
        "w_uq": f(w_uq)[0], "kv_norm_g": f(kv_norm_g)[0], "w_ukv": f(w_ukv)[0],
        "w_o_swa": f(w_o_swa)[0], "w_o_mla": f(w_o_mla)[0], "w_out": f(w_out)[0],
        "ffn_norm_g": f(ffn_norm_g)[0], "w_gate": f(w_gate)[0], "w_up": f(w_up)[0],
        "w_down": f(w_down)[0], "final_norm_g": f(final_norm_g),
        "sinkrep": np.ascontiguousarray(np.repeat(sinks, 128)[None, :]),
        "sinkb": np.ascontiguousarray(np.broadcast_to(sinks[None, :], (128, 8))),
    }
    shared.update(_consts())
    nc = build_nc(debug=_debug, phases=_phases)
    in_maps = []
    for b in range(8):
        m = dict(shared)
        m["x"] = np.ascontiguousarray(x[b])
        in_maps.append(m)
    res = run_bass_kernel_spmd(nc, in_maps, core_ids=list(range(8)))
    if _debug:
        return res.results
    return np.stack([np.asarray(r["out"], dtype=np.float32) for r in res.results], axis=0)
```
